# Optimizing a Trainium2 kernel written in Bass

```python
import math
import jax
import jax.numpy as jnp
from jax import lax
import numpy as np

D_MODEL = 1024
BATCH = 8
SEQ = 2048
DEPTH = 2
DEC_BATCH = 128
DEC_SEQ = 8
PAST_LEN = 16384
PAGE_SIZE = 128

MIX_WIDTH = D_MODEL
HGRN_WIDTH = MIX_WIDTH // 2
CONV_WIDTH = MIX_WIDTH - HGRN_WIDTH
HGRN_HEAD_DIM = 128
HGRN_HEADS = HGRN_WIDTH // HGRN_HEAD_DIM
CONV_K = 3
D_FF = 4 * D_MODEL
CHUNK = 64
PROJ_WIDTH = 4 * HGRN_WIDTH + 3 * CONV_WIDTH
SPLITS = (HGRN_WIDTH, 2 * HGRN_WIDTH, 3 * HGRN_WIDTH, 4 * HGRN_WIDTH,
          4 * HGRN_WIDTH + CONV_WIDTH, 4 * HGRN_WIDTH + 2 * CONV_WIDTH)
ALPHA = float((2 * DEPTH) ** 0.25)
BETA = float((8 * DEPTH) ** -0.25)
LN_EPS = 1e-5
RMS_EPS = 1e-6

kernel_name = "hgrn2_shortconv_hybrid_step"


def _layer_norm(x, g, b):
    xf = x.astype(jnp.float32)
    mu = jnp.mean(xf, axis=-1, keepdims=True)
    xc = xf - mu
    var = jnp.mean(xc * xc, axis=-1, keepdims=True)
    y = xc * lax.rsqrt(var + LN_EPS) * g.astype(jnp.float32) + b.astype(jnp.float32)
    return y.astype(x.dtype)


def _hgrn2(q, log_f, k, v, s0):
    bsz, t_len, n_h, d_k = q.shape
    d_v = v.shape[-1]
    c = math.gcd(t_len, CHUNK)
    n = t_len // c

    def to_chunks(a):
        return jnp.moveaxis(a.reshape(bsz, n, c, *a.shape[2:]), 1, 0)

    causal = jnp.tril(jnp.ones((c, c), dtype=bool))[None, :, :, None, None]

    def step(s_prev, inp):
        qc, lfc, kc, vc = inp
        b = jnp.cumsum(lfc, axis=1)
        o_inter = jnp.einsum('bthk,bhkv->bthv', qc * jnp.exp(b), s_prev)
        diff = b[:, :, None] - b[:, None, :]
        decay = jnp.exp(jnp.where(causal, diff, -jnp.inf))
        scores = jnp.einsum('bthk,btshk,bshk->bhts', qc, decay, kc)
        o_intra = jnp.einsum('bhts,bshv->bthv', scores, vc)
        b_last = b[:, -1]
        k_dec = kc * jnp.exp(b_last[:, None] - b)
        s_new = jnp.exp(b_last)[..., None] * s_prev + jnp.einsum('bshk,bshv->bhkv', k_dec, vc)
        return s_new, o_inter + o_intra

    s_fin, o = lax.scan(step, s0, (to_chunks(q), to_chunks(log_f), to_chunks(k), to_chunks(v)))
    o = jnp.moveaxis(o, 0, 1).reshape(bsz, t_len, n_h, d_v)
    return o, s_fin


def _layer(x, s_hgrn, conv_buf, lb, w_in, conv_w, onorm_g, w_out,
           ln1_g, ln1_b, w_ff1, w_ff2, ln2_g, ln2_b):
    bsz, t_len, _ = x.shape
    f32 = jnp.float32
    proj = jnp.einsum('btd,dp->btp', x, w_in)
    q, f_pre, i_val, g, gate_b, gate_c, h = jnp.split(proj, SPLITS, axis=-1)

    shp = (bsz, t_len, HGRN_HEADS, HGRN_HEAD_DIM)
    qh = jax.nn.silu(q.astype(f32)).reshape(shp)
    lbh = lb.reshape(HGRN_HEADS, HGRN_HEAD_DIM)
    log_f = jnp.logaddexp(jnp.log(lbh), jnp.log1p(-lbh) + jax.nn.log_sigmoid(f_pre.astype(f32).reshape(shp)))
    kh = -jnp.expm1(log_f)
    vh = i_val.astype(f32).reshape(shp)
    o, s_new = _hgrn2(qh, log_f, kh, vh, s_hgrn.astype(f32))
    o = o * lax.rsqrt(jnp.mean(o * o, axis=-1, keepdims=True) + RMS_EPS)
    o = o.reshape(bsz, t_len, HGRN_WIDTH) * onorm_g.astype(f32) * jax.nn.silu(g.astype(f32))

    u = gate_c * h
    full = jnp.concatenate([conv_buf.astype(u.dtype), u], axis=1)
    conv_out = sum(conv_w[j] * full[:, j:j + t_len] for j in range(CONV_K))
    yc = gate_b * conv_out
    new_buf = full[:, t_len:]

    mix = jnp.einsum('btm,md->btd', jnp.concatenate([o.astype(x.dtype), yc], axis=-1), w_out)
    x = _layer_norm(ALPHA * x + mix, ln1_g, ln1_b)

    hid = jnp.square(jax.nn.relu(jnp.einsum('btd,df->btf', x, w_ff1)))
    x = _layer_norm(ALPHA * x + jnp.einsum('btf,fd->btd', hid, w_ff2), ln2_g, ln2_b)
    return x, s_new, new_buf


def setup_inputs(seed: int = 0) -> dict:
    key = jax.random.key(seed)
    ks = jax.random.split(key, 16)
    nrm = jax.random.normal
    x_prompt = nrm(ks[0], (BATCH, SEQ, D_MODEL), jnp.float32)
    x_sample = nrm(ks[1], (DEC_BATCH, DEC_SEQ, D_MODEL), jnp.float32)
    state_hgrn = 0.5 * nrm(ks[2], (DEPTH, DEC_BATCH, HGRN_HEADS, HGRN_HEAD_DIM, HGRN_HEAD_DIM), jnp.float32)
    state_conv = 0.5 * nrm(ks[3], (DEPTH, DEC_BATCH, CONV_K - 1, CONV_WIDTH), jnp.float32)
    col_scale = jnp.concatenate([
        jnp.ones((2 * HGRN_WIDTH,), jnp.float32), jnp.full((HGRN_WIDTH,), BETA, jnp.float32),
        jnp.ones((HGRN_WIDTH + 2 * CONV_WIDTH,), jnp.float32), jnp.full((CONV_WIDTH,), BETA, jnp.float32)])
    w_in = nrm(ks[4], (DEPTH, D_MODEL, PROJ_WIDTH), jnp.float32) * (D_MODEL ** -0.5) * col_scale
    lb_logits = nrm(ks[5], (DEPTH, HGRN_WIDTH), jnp.float32)
    conv_w = nrm(ks[6], (DEPTH, CONV_K, CONV_WIDTH), jnp.float32) * (CONV_K ** -0.5)
    onorm_g = 1.0 + 0.01 * nrm(ks[7], (DEPTH, HGRN_WIDTH), jnp.float32)
    w_out = nrm(ks[8], (DEPTH, MIX_WIDTH, D_MODEL), jnp.float32) * (MIX_WIDTH ** -0.5) * BETA
    ln1_g = 1.0 + 0.01 * nrm(ks[9], (DEPTH, D_MODEL), jnp.float32)
    ln1_b = 0.01 * nrm(ks[10], (DEPTH, D_MODEL), jnp.float32)
    w_ff1 = nrm(ks[11], (DEPTH, D_MODEL, D_FF), jnp.float32) * (D_MODEL ** -0.5) * BETA
    w_ff2 = nrm(ks[12], (DEPTH, D_FF, D_MODEL), jnp.float32) * (D_FF ** -0.5) * BETA
    ln2_g = 1.0 + 0.01 * nrm(ks[13], (DEPTH, D_MODEL), jnp.float32)
    ln2_b = 0.01 * nrm(ks[14], (DEPTH, D_MODEL), jnp.float32)
    return {"x_prompt": x_prompt, "x_sample": x_sample,
            "state_hgrn": state_hgrn, "state_conv": state_conv,
            "w_in": w_in, "lb_logits": lb_logits, "conv_w": conv_w, "onorm_g": onorm_g,
            "w_out": w_out, "ln1_g": ln1_g, "ln1_b": ln1_b,
            "w_ff1": w_ff1, "w_ff2": w_ff2, "ln2_g": ln2_g, "ln2_b": ln2_b}


def reference(x_prompt, x_sample, state_hgrn, state_conv, w_in, lb_logits, conv_w, onorm_g,
              w_out, ln1_g, ln1_b, w_ff1, w_ff2, ln2_g, ln2_b):
    p = jax.nn.softmax(lb_logits.astype(jnp.float32), axis=0)
    cum = jnp.cumsum(p, axis=0)
    lower_bounds = cum - cum[0:1]

    yp, ys = x_prompt, x_sample
    hp_list, cp_list, hs_list, cs_list = [], [], [], []
    for l in range(DEPTH):
        weights = (w_in[l], conv_w[l], onorm_g[l], w_out[l], ln1_g[l], ln1_b[l],
                   w_ff1[l], w_ff2[l], ln2_g[l], ln2_b[l])
        s0 = jnp.zeros((BATCH, HGRN_HEADS, HGRN_HEAD_DIM, HGRN_HEAD_DIM), jnp.float32)
        b0 = jnp.zeros((BATCH, CONV_K - 1, CONV_WIDTH), x_prompt.dtype)
        yp, hp, cp = _layer(yp, s0, b0, lower_bounds[l], *weights)
        ys, hs, cs = _layer(ys, state_hgrn[l], state_conv[l], lower_bounds[l], *weights)
        hp_list.append(hp.astype(x_prompt.dtype))
        cp_list.append(cp.astype(x_prompt.dtype))
        hs_list.append(hs.astype(state_hgrn.dtype))
        cs_list.append(cs.astype(state_conv.dtype))
    new_hgrn_prompt = jnp.stack(hp_list, axis=0)
    new_conv_prompt = jnp.stack(cp_list, axis=0)
    new_hgrn_sample = jnp.stack(hs_list, axis=0)
    new_conv_sample = jnp.stack(cs_list, axis=0)
    return (yp, ys, new_hgrn_prompt, new_conv_prompt, new_hgrn_sample, new_conv_sample)
```

```python
from contextlib import ExitStack

import numpy as np
import concourse.bass as bass
import concourse.mybir as mybir
from concourse.bass_utils import run_bass_kernel_spmd

F32 = mybir.dt.float32
BF16 = mybir.dt.bfloat16
AF = mybir.ActivationFunctionType
ALU = mybir.AluOpType

N_CORES = 8
D = 1024
DEPTH = 2
SEQ = 2048
DEC_B = 128
DEC_T = 8
HW = 512
CW_ = 512
NH = 4
PW = 3584
DFF = 4096
ALPHA = float((2 * DEPTH) ** 0.25)
LN_EPS = 1e-5
RMS_EPS = 1e-6
N_DMA_SEMS = 16
SCHEDULE = True
import os as _os
AHEAD_RESERVE = 0
SYNC_LAT = float(_os.environ.get('SYNC_LAT', '250'))
import os as _os
LN2_ENG = tuple(_os.environ.get('LN2_ENG', 'dve,pool').split(','))
KTOK_ENG = _os.environ.get('KTOK_ENG', 'act')
KM_ENG = _os.environ.get('KM_ENG', 'dve')
CAR_ENG = _os.environ.get('CAR_ENG', 'dve')
LN1_ENG = tuple(_os.environ.get('LN1_ENG', 'pool').split(','))
SIM_NOFENCE = bool(_os.environ.get('SIM_NOFENCE'))
SIM_DB = _os.environ.get('SIM_DB', '')


class LazyAP:
    phys = None
    bankmap = None

    def __init__(self, tid, ops=()):
        self.tid = tid
        self.ops = ops

    def __getitem__(self, key):
        return LazyAP(self.tid, self.ops + (("g", key),))

    def rearrange(self, s, **kw):
        return LazyAP(self.tid, self.ops + (("r", s, kw),))

    def bitcast(self, dt):
        return LazyAP(self.tid, self.ops + (("b", dt),))

    def _apply(self, t):
        for o in self.ops:
            if o[0] == "g":
                t = t[o[1]]
            elif o[0] == "r":
                t = t.rearrange(o[1], **o[2])
            else:
                t = t.bitcast(o[1])
        return t

    @property
    def shape(self):
        return self._apply(LazyAP.phys[0]).shape

    def resolve(self):
        return self._apply(LazyAP.phys[LazyAP.bankmap[self.tid]])


class _LazyBanks:
    def __getitem__(self, tid):
        return LazyAP(tid)


def rs(x):
    return x.resolve() if isinstance(x, LazyAP) else x


class _Op:
    __slots__ = ("eng", "fn", "dma", "idx", "deps", "sig", "semval", "dsem", "dval", "prev_dval", "deps_all",
                 "cost", "nbytes", "desc", "cost_start", "cost_fin")

    def __init__(self, eng, fn, dma, idx):
        self.eng = eng
        self.fn = fn
        self.dma = dma
        self.idx = idx
        self.deps = ()
        self.deps_all = ()
        self.cost = 100.0
        self.nbytes = 0
        self.sig = False
        self.semval = 0
        self.dsem = None
        self.dval = 0
        self.prev_dval = 0


class Sched:
    ENGS = ("pe", "act", "dve", "pool", "sp")

    def __init__(self):
        self.ops = []
        self.last_w = {}
        self.readers = {}
        self.out_dmas = []
        self.ranges = {}
        self.alias = None
        self.subkeys = {}
        self.epoch = {}
        self.absorbed = {}

    def reg(self, name, lo, hi):
        if name in self.ranges:
            a, b = self.ranges[name]
            lo, hi = min(a, lo), max(b, hi)
        self.ranges[name] = (lo, hi)

    def _aliases(self, key):
        name = key if isinstance(key, str) else key[0]
        if self.alias is None:
            self.alias = {}
            items = list(self.ranges.items())
            for n, (lo, hi) in items:
                self.alias[n] = [m for m, (a, b) in items if m != n and a < hi and lo < b]
        self.subkeys.setdefault(name, set()).add(key)
        out = []
        for m in self.alias.get(name, ()):
            out.extend(self.subkeys.get(m, ()))
        return out

    def add(self, eng, fn, reads=(), writes=(), dma=False, is_output=False, cost=100.0, nbytes=0):
        if SIM_DB and getattr(self, "parity", None) is not None:
            ren = lambda k: (k, self.parity) if (k if isinstance(k, str) else k[0]) in SIM_DB.split(",") else k
            reads = [ren(k) for k in reads]
            writes = [ren(k) for k in writes]
        idx = len(self.ops)
        psr = [k for k in reads if isinstance(k, tuple) and k[0] == "ps"]
        if psr:
            writes = list(writes) + [k for k in psr if k not in writes]
        op = _Op(eng, fn, dma, idx)
        op.cost = cost
        op.nbytes = nbytes
        op.desc = (tuple(reads), tuple(writes))
        deps = set()
        for k in reads:
            self._aliases(k)
            w = self.last_w.get(k)
            if w is not None:
                deps.add(w)
        for k in writes:
            w = self.last_w.get(k)
            if w is not None:
                deps.add(w)
            for r in self.readers.get(k, ()):
                deps.add(r)
            for k2 in self._aliases(k):
                ep = self.epoch.get(k2, 0)
                if self.absorbed.get((k, k2)) == ep:
                    continue
                self.absorbed[(k, k2)] = ep
                w = self.last_w.get(k2)
                if w is not None:
                    deps.add(w)
                for r in self.readers.get(k2, ()):
                    deps.add(r)
        deps.discard(idx)
        for k in reads:
            self.readers.setdefault(k, []).append(idx)
            self.epoch[k] = self.epoch.get(k, 0) + 1
        for k in writes:
            self.last_w[k] = idx
            self.readers[k] = []
            self.epoch[k] = self.epoch.get(k, 0) + 1
        op.deps_all = tuple(sorted(deps))
        op.deps = tuple(d for d in op.deps_all
                        if not (eng == "pe" and not dma and self.ops[d].eng == "pe" and not self.ops[d].dma))
        self.ops.append(op)
        if is_output:
            self.out_dmas.append(idx)
        return idx

    def schedule(self, window=48, sync_lat=SYNC_LAT, dma_lat=2200.0, dma_bw=180.0):
        ops = self.ops
        n = len(ops)
        succ = [[] for _ in range(n)]
        cnt = [0] * n
        for op in ops:
            cnt[op.idx] = len(op.deps_all)
            for d in op.deps_all:
                succ[d].append(op.idx)
        rtime = [0.0] * n
        crit_dep = {}
        op_prev_on_eng = {}
        last_on_eng = {}
        self.crit_dep = crit_dep
        self.op_prev_on_eng = op_prev_on_eng
        tile_ops = {}
        for op in ops:
            for k in set(op.desc[0]) | set(op.desc[1]):
                if isinstance(k, tuple) and k[0] == "ps":
                    tile_ops.setdefault(k[1], []).append(op.idx)
        alloc_order = sorted(tile_ops, key=lambda t: tile_ops[t][0])
        first_of = {tile_ops[t][0]: t for t in alloc_order}
        op_tiles = {}
        for t, lst in tile_ops.items():
            for i in lst:
                op_tiles.setdefault(i, []).append(t)
        next_alloc = [0]
        NB = 8
        ev = []
        for t, lst in tile_ops.items():
            ev.append((lst[0], 1))
            ev.append((lst[-1] + 0.5, -1))
        live = mx = 0
        for _, d in sorted(ev):
            live += d
            mx = max(mx, live)
        ahead_max = max(0, NB - mx - AHEAD_RESERVE)
        self.max_live = mx
        rank = {t: i for i, t in enumerate(alloc_order)}
        allocated = set()
        tenant = [None] * NB
        bank_free = [0.0] * NB
        remaining = {t: len(lst) for t, lst in tile_ops.items()}
        tile_fin = {t: 0.0 for t in tile_ops}
        bankmap = {}
        pend = {e: [] for e in self.ENGS}
        for op in ops:
            pend[op.eng].append(op.idx)
        head = {e: 0 for e in self.ENGS}
        done = {e: [False] * len(pend[e]) for e in self.ENGS}
        etime = {e: 0.0 for e in self.ENGS}
        dma_free = [0.0]
        order = {e: [] for e in self.ENGS}
        left = n
        tmax = 0.0
        while left:
            best = None
            for e in self.ENGS:
                lst = pend[e]
                dn = done[e]
                h = head[e]
                while h < len(lst) and dn[h]:
                    h += 1
                head[e] = h
                seen = 0
                k = h
                et = etime[e]
                while k < len(lst) and seen < window:
                    if not dn[k]:
                        seen += 1
                        i = lst[k]
                        if cnt[i] == 0:
                            rdy = rtime[i] if rtime[i] > et else et
                            t_new = first_of.get(i)
                            if t_new is not None:
                                if alloc_order[next_alloc[0]] != t_new:
                                    n_ahead = sum(1 for b in range(NB) if tenant[b] is not None
                                                  and remaining[tenant[b]] > 0 and rank[tenant[b]] > next_alloc[0])
                                    if n_ahead >= ahead_max:
                                        k += 1
                                        continue
                                fb = None
                                for b in range(NB):
                                    if tenant[b] is None or remaining[tenant[b]] == 0:
                                        if fb is None or bank_free[b] < bank_free[fb]:
                                            fb = b
                                if fb is None:
                                    k += 1
                                    continue
                                if bank_free[fb] + sync_lat > rdy:
                                    rdy = bank_free[fb] + sync_lat
                            cand = (rdy + 0.02 * (k - h), i, e, k, rdy)
                            if best is None or cand < best:
                                best = cand
                    k += 1
            assert best is not None, "scheduler deadlock"
            _, idx, e, k, rdy = best
            op = ops[idx]
            t_new = first_of.get(idx)
            if t_new is not None:
                fb = None
                for b in range(NB):
                    if tenant[b] is None or remaining[tenant[b]] == 0:
                        if fb is None or bank_free[b] < bank_free[fb]:
                            fb = b
                prev = tenant[fb]
                if prev is not None:
                    extra = tuple(tile_ops[prev])
                    op.deps_all = tuple(sorted(set(op.deps_all) | set(extra)))
                    op.deps = tuple(d for d in op.deps_all
                                    if not (op.eng == "pe" and not op.dma and ops[d].eng == "pe" and not ops[d].dma))
                tenant[fb] = t_new
                bankmap[t_new] = fb
                allocated.add(t_new)
                while next_alloc[0] < len(alloc_order) and alloc_order[next_alloc[0]] in allocated:
                    next_alloc[0] += 1
            if op.dma:
                if op.nbytes <= 600000:
                    f = rdy + dma_lat + 2.0 * op.nbytes / dma_bw
                else:
                    st = max(rdy, dma_free[0])
                    dma_free[0] = st + op.nbytes / dma_bw
                    f = dma_free[0] + dma_lat
                etime[e] = rdy + 60.0
            else:
                f = rdy + op.cost
                etime[e] = f
            op.cost_start = rdy
            op.cost_fin = f
            if f > tmax:
                tmax = f
            for t in op_tiles.get(idx, ()):
                remaining[t] -= 1
                if f > tile_fin[t]:
                    tile_fin[t] = f
                if remaining[t] == 0:
                    bank_free[bankmap[t]] = tile_fin[t]
            for s in succ[idx]:
                cnt[s] -= 1
                if f + sync_lat > rtime[s]:
                    rtime[s] = f + sync_lat
                    crit_dep[s] = idx
            op_prev_on_eng[idx] = last_on_eng.get(e)
            last_on_eng[e] = idx
            done[e][k] = True
            order[e].append(idx)
            left -= 1
        self.order = order
        self.sim_time = tmax
        LazyAP.bankmap = bankmap
        return order

    def emit(self, nc, sems, dma_sems):
        ops = self.ops
        order = getattr(self, "order", None)
        if order is None:
            order = {e: [op.idx for op in ops if op.eng == e] for e in self.ENGS}
        for op in ops:
            for d in op.deps:
                ops[d].sig = True
        cnt = {e: 0 for e in self.ENGS}
        dcnt = {e: 0 for e in self.ENGS}
        for op in (ops[i] for e in self.ENGS for i in order[e]):
            if op.dma:
                j = dcnt[op.eng]
                dcnt[op.eng] += 1
                pool = dma_sems[op.eng]
                op.dsem = pool[j % len(pool)]
                op.dval = 16 * (j // len(pool) + 1)
                op.prev_dval = 16 * (j // len(pool))
            elif op.sig:
                cnt[op.eng] += 1
                op.semval = cnt[op.eng]
        stats = {e: [0, 0] for e in self.ENGS}

        def run(ename, e):
            seen = {}

            def wait(sem, val):
                key = id(sem)
                if seen.get(key, 0) >= val:
                    return
                e.wait_ge(sem, val)
                seen[key] = val
                stats[ename][1] += 1

            for op in (ops[i] for i in order[ename]):
                need = {}
                for d in op.deps:
                    dop = ops[d]
                    sm, vv = (dop.dsem, dop.dval) if dop.dma else (sems[dop.eng], dop.semval)
                    if need.get(id(sm), (None, 0))[1] < vv:
                        need[id(sm)] = (sm, vv)
                for sm, vv in need.values():
                    wait(sm, vv)
                if op.dma:
                    if op.prev_dval > 0:
                        wait(op.dsem, op.prev_dval)
                    ins = op.fn(e)
                    ins.then_inc(op.dsem, 16)
                else:
                    ins = op.fn(e)
                    if op.sig:
                        ins.then_inc(sems[op.eng], 1)
                stats[ename][0] += 1
            if ename == "sp":
                for d in self.out_dmas:
                    dop = ops[d]
                    wait(dop.dsem, dop.dval)

        with nc.Block() as block:
            @block.tensor
            def _(e):
                run("pe", e)

            @block.scalar
            def _(e):
                run("act", e)

            @block.vector
            def _(e):
                run("dve", e)

            @block.gpsimd
            def _(e):
                run("pool", e)

            @block.sync
            def _(e):
                run("sp", e)
        return stats


def build_program(n_ptiles=16, taps=()):
    assert n_ptiles % 2 == 0
    NPT = n_ptiles
    NT = NPT + 1
    SEQL = NPT * 128
    nc = bass.Bass("TRN2", target_bir_lowering=False)

    def din(name, shape):
        return nc.dram_tensor(name, list(shape), F32, kind="ExternalInput").ap()

    def dout(name, shape):
        return nc.dram_tensor(name, list(shape), F32, kind="ExternalOutput").ap()

    xp = din("xp", [SEQL, D])
    xs = din("xs", [128, D])
    sh = din("sh", [DEPTH, 16, NH, 128, 128])
    sc = din("sc", [DEPTH, 16, 2, CW_])
    w_in = din("w_in", [DEPTH, D, PW])
    lb_logits = din("lb_logits", [DEPTH, HW])
    conv_w = din("conv_w", [DEPTH, 3, CW_])
    onorm_g = din("onorm_g", [DEPTH, HW])
    w_out = din("w_out", [DEPTH, D, D])
    ln1_g = din("ln1_g", [DEPTH, D])
    ln1_b = din("ln1_b", [DEPTH, D])
    w_ff1 = din("w_ff1", [DEPTH, D, DFF])
    w_ff2 = din("w_ff2", [DEPTH, DFF, D])
    ln2_g = din("ln2_g", [DEPTH, D])
    ln2_b = din("ln2_b", [DEPTH, D])

    yp = dout("yp", [SEQL, D])
    ys = dout("ys", [128, D])
    hp = dout("hp", [DEPTH, NH, 128, 128])
    cp = dout("cp", [DEPTH, 2, CW_])
    hs = dout("hs", [DEPTH, 16, NH, 128, 128])
    cs = dout("cs", [DEPTH, 16, 2, CW_])
    tap_out = {t: dout("tap_" + t, [NT, 128, D]) for t in taps if t != "mix"}
    if "mix" in taps:
        dbg_mix = dout("dbg_mix", [128, 8, 128])
        dbg_sg = dout("dbg_sg", [128, 4, 128])
        dbg_qt = dout("dbg_qt", [128, 4, 128])
        dbg_kt = dout("dbg_kt", [128, 4, 128])

    es = ExitStack()
    ARENA_BYTES = 211968
    arena = es.enter_context(nc.sbuf_tensor("arena", [128, ARENA_BYTES // 4], F32))
    LazyAP.phys = [es.enter_context(nc.psum_tensor(f"ps{i}", [128, 512], F32)) for i in range(8)]
    psb = _LazyBanks()
    sems = {e: es.enter_context(nc.semaphore(f"s_{e}")) for e in Sched.ENGS}
    dma_sems = {e: [es.enter_context(nc.semaphore(f"d_{e}{i}")) for i in range(N_DMA_SEMS if e == "sp" else 6)]
                for e in ("sp", "pool")}
    for e in ("act", "pe", "dve"):
        dma_sems[e] = dma_sems["sp"]

    class Arena:
        def __init__(self, base, limit):
            self.off = base
            self.limit = limit

        def alloc(self, shape, dt, name=None):
            nel = int(np.prod(shape[1:]))
            nbytes = nel * (4 if dt == F32 else 2)
            nbytes = (nbytes + 31) // 32 * 32
            o = self.off
            self.off += nbytes
            assert self.off <= self.limit, (self.off, self.limit, shape)
            if name is not None:
                S.reg(name, o, o + nbytes)
            v = arena[:, o // 4:(o + nbytes) // 4]
            if dt != F32:
                v = v.bitcast(dt)
            v = v[:, 0:nel]
            if len(shape) == 3:
                v = v.rearrange("p (a b) -> p a b", a=shape[1])
            elif len(shape) == 4:
                v = v.rearrange("p (a b c) -> p a b c", a=shape[1], b=shape[2])
            return v

    S = Sched()
    A0 = Arena(0, ARENA_BYTES)
    R = A0.alloc([128, NT, D], F32)
    ident = A0.alloc([128, 128], F32)
    identb = A0.alloc([128, 128], BF16)
    ones_s = A0.alloc([128, 128], BF16)
    zeros_b = A0.alloc([128, 128], BF16)
    maskP = A0.alloc([128, 128], F32)
    maskS = A0.alloc([128, 128], F32)
    RMP = A0.alloc([128, 256], F32)
    RMS = A0.alloc([128, 128], F32)
    GMP = A0.alloc([128, 2], F32)
    GMS = A0.alloc([128, 16], F32)
    LBR = A0.alloc([128, DEPTH, NH], F32)
    LB = A0.alloc([128, DEPTH, NH], F32)
    OML = A0.alloc([128, DEPTH, NH], F32)
    LNOML = A0.alloc([128, DEPTH, NH], F32)
    LBT = A0.alloc([128, NH], F32)
    CWT = A0.alloc([128, DEPTH, 3, 4], F32)
    ONG = A0.alloc([128, DEPTH, NH], F32)
    SCT = A0.alloc([128, DEPTH * 4 * 16, 2], F32)
    CTAIL = A0.alloc([128, 4 * 16, 2], F32)
    CAR = A0.alloc([128, 4, 2], F32)
    FSCR = A0.alloc([128, 8], F32)
    G1c = A0.alloc([128, DEPTH, 8], F32)
    B1c = A0.alloc([128, DEPTH, 8], F32)
    LNG = A0.alloc([128, D], F32)
    LNB = A0.alloc([128, D], F32)
    LNS = [dict(ST=A0.alloc([128, 12], F32), MV=A0.alloc([128, 2], F32), LNV=A0.alloc([128, 1], F32),
                RSTD=A0.alloc([128, 1], F32), NMR=A0.alloc([128, 1], F32)) for _ in range(4)]
    persist_end = A0.off
    XT_BYTES = 8 * NT * 128 * 2
    P0 = persist_end
    NS = 256
    AM = Arena(P0, ARENA_BYTES)
    WINB = [AM.alloc([128, 8, 512], BF16, "WIN%d" % b) for b in range(7)]
    WOUT = AM.alloc([128, 8, D], BF16, "WOUT")
    XTc = AM.alloc([128, 8, NS], BF16, "XTc")
    QT = AM.alloc([128, NH, NS], BF16, "QT")
    KT = AM.alloc([128, NH, NS], BF16, "KT")
    KTOK = AM.alloc([128, 2, 512], BF16, "KTOK")
    VTOK = AM.alloc([128, 2, 512], BF16, "VTOK")
    GTALL = AM.alloc([128, 6 * NS], F32, "GT")
    GT = [GTALL[:, i * NS:(i + 1) * NS] for i in range(6)]
    RST = GTALL[:, 0:512]
    TMPO = GTALL[:, 512:1024]
    SG = AM.alloc([128, NH, NS], F32, "SG")
    ELb = AM.alloc([128, NH, 16], F32, "ELb")
    early_end = AM.off
    ATm = [AM.alloc([128, NH, 128], BF16, "ATm") for _ in range(2)]
    KM = [AM.alloc([128, 512], BF16, "KM") for _ in range(2)]
    Sst = AM.alloc([128, NH, 128], F32, "Sst")
    STMP = AM.alloc([128, NH, 128], F32, "STMP")
    Sb = [AM.alloc([128, NH, 128], BF16, "Sb") for _ in range(3)]
    S0f = [AM.alloc([128, NH, 128], F32, "S0f") for _ in range(2)]
    S0b = [AM.alloc([128, NH, 128], BF16, "S0b") for _ in range(2)]
    SN = [AM.alloc([128, NH, 128], F32, "SN") for _ in range(2)]
    late0 = AM.off
    OSQ = AM.alloc([128, 512], BF16, "OSQ")
    MIX = AM.alloc([128, 8, NS], BF16, "MIX")
    U = AM.alloc([128, NS + 8], F32, "U")
    mixer_end = AM.off
    AF_ = Arena(P0, ARENA_BYTES)
    F1 = [None] * 3
    F2 = [None] * 3
    F1[0] = AF_.alloc([128, 8, 512], BF16, "F1_0")
    F2[0] = AF_.alloc([128, 4, D], BF16, "F2_0")
    XTt = [AF_.alloc([128, 8, 128], BF16, "XT%d" % i_) for i_ in range(NT)]
    xt_off = [(P0 + 2 * 8192 + i_ * 2048) // 2 for i_ in range(NT)]
    SQ = [AF_.alloc([128, 512], F32, "SQ") for _ in range(2)]
    assert AF_.off <= P0 + 7 * 8192, AF_.off
    AF_.off = P0 + 7 * 8192
    F1[1] = AF_.alloc([128, 8, 512], BF16, "F1_1")
    F2[1] = AF_.alloc([128, 4, D], BF16, "F2_1")
    F1[2] = AF_.alloc([128, 8, 512], BF16, "F1_2")
    F2[2] = AF_.alloc([128, 4, D], BF16, "F2_2")
    assert AF_.off <= early_end, (AF_.off, early_end)
    AF_.off = late0
    HID = [AF_.alloc([128, 4, 512], BF16, "HID") for _ in range(2)]
    ffn_end = AF_.off
    print(f"[kernel] arena: persist={persist_end} mixer_end={mixer_end} ffn_end={ffn_end} limit={ARENA_BYTES}")

    bank_ctr = [0]

    SIM_INFBANK = bool(_os.environ.get("SIM_INFBANK"))
    uniq = [0]

    cur_uid = {}

    def bank(cls=None):
        uniq[0] += 1
        return uniq[0]

    def fsz(ap):
        return int(np.prod(ap.shape[1:]))

    def mm(out, lhsT, rhs, start, stop, reads, writes):
        S.add("pe", lambda e: e.matmul(rs(out), lhsT=lhsT, rhs=rhs, start=start, stop=stop), reads, writes,
              cost=max(64, fsz(rhs)) / 2.4 + 8.0)

    def tr(out, in_, idn, reads, writes):
        S.add("pe", lambda e: e.transpose(rs(out), in_, idn), reads, writes, cost=110.0)

    def act(out, in_, func, reads, writes, scale=None, bias=None):
        kw = {}
        if scale is not None:
            kw["scale"] = scale
        if bias is not None:
            kw["bias"] = bias
        S.add("act", lambda e: e.activation(out=out, in_=rs(in_), func=func, **kw), reads, writes,
              cost=220.0 + 0.83 * fsz(in_))

    def ecost(eng, n, f):
        return (100.0 + 1.04 * f * n) * (2.0 if eng == "pool" else 1.0)

    def tt(out, a, b, op, reads, writes, eng="dve", f=1.5):
        S.add(eng, lambda e: e.tensor_tensor(out=out, in0=rs(a), in1=rs(b), op=op), reads, writes,
              cost=ecost(eng, fsz(out), f))

    def ts(out, in0, s1, op0, reads, writes, s2=None, op1=None, eng="dve"):
        c = ecost(eng, fsz(out), 1.0)
        if op1 is None:
            S.add(eng, lambda e: e.tensor_scalar(out=out, in0=rs(in0), scalar1=s1, scalar2=None, op0=op0), reads, writes,
                  cost=c)
        else:
            S.add(eng, lambda e: e.tensor_scalar(out=out, in0=rs(in0), scalar1=s1, scalar2=s2, op0=op0, op1=op1),
                  reads, writes, cost=c)

    def stt(out, in0, scalar, in1, op0, op1, reads, writes):
        S.add("dve", lambda e: e.scalar_tensor_tensor(out=out, in0=rs(in0), scalar=scalar, in1=rs(in1), op0=op0,
                                                        op1=op1),
              reads, writes, cost=ecost("dve", fsz(out), 1.3))

    def cp_(out, in_, reads, writes, eng="dve"):
        if eng == "act":
            S.add("act", lambda e: e.copy(out=out, in_=rs(in_)), reads, writes, cost=220.0 + 0.83 * fsz(out))
        else:
            S.add(eng, lambda e: e.tensor_copy(out=out, in_=rs(in_)), reads, writes, cost=ecost(eng, fsz(out), 1.0))

    def memset(ap, val, reads, writes, eng="pool"):
        S.add(eng, lambda e: e.memset(ap, val), reads, writes, cost=ecost(eng, fsz(ap), 1.0))

    def dma(eng, out, in_, reads, writes, is_output=False, **kw):
        nb = 128 * fsz(out) * 4
        S.add(eng, lambda e: e.dma_start(out=out, in_=in_, **kw), reads, writes, dma=True, is_output=is_output,
              nbytes=nb)

    def asel(ap, pattern, cm, base, reads, writes):
        S.add("pool", lambda e: e.affine_select(out=ap, in_=ap, pattern=pattern, compare_op=ALU.is_ge, fill=0.0,
                                                 base=base, channel_multiplier=cm), reads, writes, cost=400.0)

    PS = lambda b: ("ps", b)

    memset(ident, 1.0, [], ["ident"])
    S.add("pool", lambda e: e.affine_select(out=ident, in_=ident, pattern=[[-1, 128]], compare_op=ALU.is_equal,
                                             fill=0.0, base=0, channel_multiplier=1), ["ident"], ["ident"])
    cp_(identb, ident, ["ident"], ["identb"], eng="pool")
    memset(ones_s, 1.0 / 128.0, [], ["ones_s"])
    memset(zeros_b, 0.0, [], ["zeros_b"])
    for (mk, blk, nm) in ((maskP, 64, "maskP"), (maskS, 8, "maskS")):
        nb = 128 // blk
        memset(mk, 1.0, [], [nm])
        v3 = mk.rearrange("p (a b) -> p a b", b=blk)
        asel(v3, [[blk, nb], [1, blk]], -1, 0, [nm], [nm])
        asel(v3, [[-blk, nb], [0, blk]], 1, 0, [nm], [nm])
    memset(RMP, 1.0, [], ["RMP"])
    memset(RMP.rearrange("p (a b) -> p a b", b=64)[:, :, 0:1], 0.0, ["RMP"], ["RMP"])
    memset(RMS, 1.0, [], ["RMS"])
    memset(RMS.rearrange("p (a b) -> p a b", b=8)[:, :, 0:1], 0.0, ["RMS"], ["RMS"])
    for (gm, blk, nm) in ((GMP, 64, "GMP"), (GMS, 8, "GMS")):
        nb = 128 // blk
        memset(gm, 1.0, [], [nm])
        asel(gm, [[-blk, nb]], 1, 0, [nm], [nm])
        asel(gm, [[blk, nb]], -1, blk - 1, [nm], [nm])
    STG = LNG
    STGK = [("STGR", i_) for i_ in range(6)]
    memset(STG[0:96, :], 0.0, [], ["LNG"] + STGK, eng="dve")
    dma("sp", STG[0:2, 0:512], lb_logits, [], [STGK[0]])
    dma("sp", STG[2:4, 0:512], onorm_g, [], [STGK[1]])
    dma("sp", STG[4:10, 0:512], conv_w.rearrange("l j c -> (l j) c"), [], [STGK[2]])
    dma("sp", STG[10:12, :], ln1_g, [], [STGK[3]])
    dma("sp", STG[12:14, :], ln1_b, [], [STGK[4]])
    dma("sp", STG[16:80, 0:512], sc.rearrange("l b j c -> (l b j) c"), [], [STGK[5]])
    for k in range(8):
        pb = bank("xt")
        pv = psb[pb][:, 0:80]
        tr(pv, STG[0:80, k * 128:(k + 1) * 128], ident[0:80, 0:80], ["LNG", "ident"] + STGK, [PS(pb)])
        cp_(G1c[:, :, k], psb[pb][:, 10:12], [PS(pb)], ["G1c"])
        cp_(B1c[:, :, k], psb[pb][:, 12:14], [PS(pb)], ["B1c"])
        if k < 4:
            cp_(LBR[:, :, k], psb[pb][:, 0:2], [PS(pb)], ["LBR"])
            cp_(ONG[:, :, k], psb[pb][:, 2:4], [PS(pb)], ["ONG"])
            cp_(CWT[:, :, :, k], psb[pb][:, 4:10].rearrange("p (a b) -> p a b", a=DEPTH), [PS(pb)], ["CWT"])
            for l_ in range(DEPTH):
                cp_(SCT[:, (l_ * 4 + k) * 16:(l_ * 4 + k + 1) * 16, :],
                    psb[pb][:, 16 + l_ * 32:16 + (l_ + 1) * 32].rearrange("p (a b) -> p a b", a=16), [PS(pb)],
                    [("SCT", l_ * 4 + k)], eng="act")
    memset(LB[:, 0, :], 0.0, [], ["LB"], eng="dve")
    memset(OML[:, 0, :], 1.0, [], ["OML"], eng="dve")
    memset(LNOML[:, 0, :], 0.0, [], ["LNOML"], eng="dve")
    tt(LBT, LBR[:, 0, :], LBR[:, 1, :], ALU.subtract, ["LBR"], ["LBT"])
    act(LBR[:, 0, :], LBT, AF.Exp, ["LBT"], ["LBR"])
    act(LBR[:, 0, :], LBR[:, 0, :], AF.Ln, ["LBR"], ["LBR"], bias=1.0)
    act(LB[:, 1, :], LBR[:, 0, :], AF.Exp, ["LBR", "LB"], ["LB"], scale=-1.0)
    tt(LNOML[:, 1, :], LBT, LBR[:, 0, :], ALU.subtract, ["LBT", "LBR", "LNOML"], ["LNOML"])
    act(OML[:, 1, :], LNOML[:, 1, :], AF.Exp, ["LNOML", "OML"], ["OML"])

    for i in range(2):
        dma("sp", R[:, i, :], xp[i * 128:(i + 1) * 128, :], [], [("R", i)])

    def late_input_loads():
        for i in range(2, NPT):
            dma("sp", R[:, i, :], xp[i * 128:(i + 1) * 128, :], ["WIN4"], [("R", i)])
        dma("sp", R[:, NPT, :], xs, ["WIN4"], [("R", NPT)])

    ln_ctr = [0]

    def layer_norm_tile(i, l, final_out, affine_eng="pool", affine=True):
        k = ln_ctr[0] % 4
        ln_ctr[0] += 1
        b = LNS[k]
        lk = ("LNS", k)
        rk = ("R", i)
        Ri = R[:, i, :]
        S.add("dve", lambda e: e.bn_stats(out=b["ST"][:, 0:6], in_=Ri[:, 0:512]), [rk], [lk], cost=640.0)
        S.add("dve", lambda e: e.bn_stats(out=b["ST"][:, 6:12], in_=Ri[:, 512:1024]), [rk, lk], [lk], cost=640.0)
        S.add("dve", lambda e: e.bn_aggr(out=b["MV"], in_=b["ST"]), [lk], [lk], cost=200.0)
        act(b["LNV"], b["MV"][:, 1:2], AF.Ln, [lk], [lk], bias=LN_EPS)
        act(b["RSTD"], b["LNV"], AF.Exp, [lk], [lk], scale=-0.5)
        stt(b["NMR"], b["MV"][:, 0:1], -1.0, b["RSTD"], ALU.mult, ALU.mult, [lk], [lk])
        act(Ri, Ri, AF.Identity, [rk, lk], [rk], scale=b["RSTD"], bias=b["NMR"])
        if affine:
            tt(Ri, Ri, LNG, ALU.mult, [rk, "LNG"], [rk], eng=affine_eng, f=1.0)
            tt(Ri, Ri, LNB, ALU.add, [rk, "LNB"], [rk], eng=affine_eng, f=1.0)
        if final_out:
            if i < NPT:
                dma("sp", yp[i * 128:(i + 1) * 128, :], Ri, [rk], [], is_output=True)
            else:
                dma("sp", ys, Ri, [rk], [], is_output=True)

    def gen_xt(i, dst, dst_key, affine_l=None):
        for half in range(2):
            pb = bank("xt")
            for q in range(4):
                dk = half * 4 + q
                tr(psb[pb][:, q * 128:(q + 1) * 128], R[:, i, dk * 128:(dk + 1) * 128], ident,
                   [("R", i), "ident"], [PS(pb)])
            if affine_l is None:
                act(dst[:, half * 4:(half + 1) * 4, :], psb[pb][:, 0:512].rearrange("p (a b) -> p a b", a=4), AF.Copy,
                    [PS(pb)], [dst_key])
            else:
                for q in range(4):
                    dk = half * 4 + q
                    act(dst[:, dk, :], psb[pb][:, q * 128:(q + 1) * 128], AF.Identity, [PS(pb), "G1c", "B1c"], [dst_key],
                        scale=G1c[:, affine_l, dk:dk + 1], bias=B1c[:, affine_l, dk:dk + 1])

    for l in range(DEPTH):
        for blk in (1, 0, 3, 2, 5, 6, 4):
            dma("pool", WINB[blk], w_in[l][:, blk * 512:(blk + 1) * 512].rearrange("(k p) n -> p k n", p=128), [],
                ["WIN%d" % blk], max_dma_last_dim=8192)
        dma("pool", WOUT, w_out[l].rearrange("(k p) n -> p k n", p=128), [], ["WOUT"], max_dma_last_dim=8192)
        if l == 0:
            late_input_loads()
        memset(Sst, 0.0, [], ["Sst"], eng="dve")
        memset(Sb[0], 0.0, [], [("Sb", 0)], eng="dve")
        memset(CAR, 0.0, [], ["CAR"], eng="dve")
        sb_ctr = [0]

        supers = [("p", 2 * i, 2) for i in range(NPT // 2)] + [("s", NPT, 1)]
        for (kind, tile0, ntl) in supers:
            S.parity = (tile0 // 2) % 2
            n = 128 * ntl
            is_p = kind == "p"
            blk = 64 if is_p else 8
            G = n // blk
            gpt = 128 // blk
            RM = RMP if is_p else RMS
            GM = GMP if is_p else GMS
            mask = maskP if is_p else maskS
            for j in range(ntl):
                gen_xt(tile0 + j, XTc[:, :, j * 128:(j + 1) * 128], "XTc")
            for h in range(NH):
                tset = (h % 2) * 3
                TA, TB, TC = GT[tset], GT[tset + 1], GT[tset + 2]
                ka, kb, kc = ("GT", tset), ("GT", tset + 1), ("GT", tset + 2)
                pb = bank("fq")
                zf = psb[pb][:, 0:n]
                zq = psb[pb][:, 256:256 + n]
                for dk in range(8):
                    mm(zf, WINB[1][:, dk, h * 128:(h + 1) * 128], XTc[:, dk, 0:n], dk == 0, dk == 7,
                       ["WIN1", "XTc"], [PS(pb)])
                for dk in range(8):
                    mm(zq, WINB[0][:, dk, h * 128:(h + 1) * 128], XTc[:, dk, 0:n], dk == 0, dk == 7,
                       ["WIN0", "XTc"], [PS(pb)])
                lb_ap = LB[:, l, h:h + 1]
                oml_ap = OML[:, l, h:h + 1]
                lno_ap = LNOML[:, l, h:h + 1]
                a, b_, c = TA[:, 0:n], TB[:, 0:n], TC[:, 0:n]
                act(a, zf, AF.Exp, [PS(pb)], [ka], scale=-1.0)
                act(b_, a, AF.Ln, [ka], [kb], bias=1.0)
                act(a, b_, AF.Exp, [kb], [ka], scale=-1.0)
                act(a, a, AF.Ln, [ka, "LB", "OML"], [ka], scale=oml_ap, bias=lb_ap)
                S.add("dve", lambda e, c=c, a=a, RMv=RM[:, 0:n]: e.tensor_tensor_scan(
                    out=c, data0=RMv, data1=a, initial=0.0, op0=ALU.mult, op1=ALU.add),
                    [ka, "RMP", "RMS"], [kc], cost=100.0 + 2.1 * n)
                tt(b_, zf, b_, ALU.add, [PS(pb), kb], [kb])
                tt(b_, b_, c, ALU.add, [kb, kc], [kb])
                act(KT[:, h, 0:n], b_, AF.Exp, [kb, "LNOML"], ["KT"], scale=-1.0, bias=lno_ap)
                act(ELb[:, h, 0:G], c.rearrange("p (g c) -> p g c", c=blk)[:, :, blk - 1], AF.Exp, [kc], ["ELb"])
                act(a, zq, AF.Exp, [PS(pb)], [ka], scale=-1.0)
                act(a, a, AF.Ln, [ka], [ka], bias=1.0)
                tt(a, c, a, ALU.subtract, [kc, ka], [ka])
                act(a, a, AF.Exp, [ka], [ka])
                tt(QT[:, h, 0:n], zq, a, ALU.mult, [PS(pb), ka], ["QT"])
            for h in range(NH):
                tset = (h % 2) * 3
                TA = GT[tset]
                ka = ("GT", tset)
                a = TA[:, 0:n]
                pb = bank("g")
                zg = psb[pb][:, 0:n]
                for dk in range(8):
                    mm(zg, WINB[3][:, dk, h * 128:(h + 1) * 128], XTc[:, dk, 0:n], dk == 0, dk == 7,
                       ["WIN3", "XTc"], [PS(pb)])
                act(a, zg, AF.Exp, [PS(pb)], [ka], scale=-1.0)
                act(a, a, AF.Ln, [ka], [ka], bias=1.0)
                act(a, a, AF.Exp, [ka], [ka], scale=-1.0)
                stt(SG[:, h, 0:n], zg, ONG[:, l, h:h + 1], a, ALU.mult, ALU.mult, [PS(pb), ka, "ONG"], ["SG"])
            for j in range(ntl):
                pb = bank("v")
                zv = psb[pb][:, 0:512]
                for dk in range(8):
                    mm(zv, XTc[:, dk, j * 128:(j + 1) * 128], WINB[2][:, dk, :], dk == 0, dk == 7,
                       ["WIN2", "XTc"], [PS(pb)])
                act(VTOK[:, j, :], zv, AF.Copy, [PS(pb)], ["VTOK"])
                pb2 = bank("kt")
                ktp = psb[pb2][:, 0:256].bitcast(BF16)
                for h in range(NH):
                    tr(ktp[:, h * 128:(h + 1) * 128], KT[:, h, j * 128:(j + 1) * 128], identb, ["KT", "identb"],
                       [PS(pb2)])
                cp_(KTOK[:, j, :], ktp, [PS(pb2)], ["KTOK"], eng=KTOK_ENG)
            def rms_gate(ov, W, sgv, mixv, okeys, three_d):
                act(OSQ[:, 0:W], ov, AF.Square, okeys, ["OSQ"])
                pbQ = bank_excl("q")
                mm(psb[pbQ][:, 0:W], ones_s, OSQ[:, 0:W], True, True, ["ones_s", "OSQ"], [PS(pbQ)])
                rst = RST[:, 0:W]
                tmp = TMPO[:, 0:W]
                act(rst, psb[pbQ][:, 0:W], AF.Ln, [PS(pbQ)], [("GT", 0), ("GT", 1)], bias=RMS_EPS)
                act(rst, rst, AF.Exp, [("GT", 0), ("GT", 1)], [("GT", 0), ("GT", 1)], scale=-0.5)
                tt(tmp, ov, rst, ALU.mult, okeys + [("GT", 0), ("GT", 1)], [("GT", 2), ("GT", 3)])
                tv = tmp.rearrange("p (a b) -> p a b", a=NH) if three_d else tmp
                tt(mixv, sgv, tv, ALU.mult, ["SG", ("GT", 2), ("GT", 3)], ["MIX"])

            held = set()

            def bank_excl(cls=None):
                while True:
                    b = bank(cls if is_p else None)
                    if b not in held:
                        return b

            for j in range(ntl):
                c0, c1 = j * 128, (j + 1) * 128
                pbA = bank_excl("sc")
                for h in range(NH):
                    mm(psb[pbA][:, h * 128:(h + 1) * 128], KT[:, h, c0:c1], QT[:, h, c0:c1], True, True,
                       ["KT", "QT"], [PS(pbA)])
                at = ATm[j % 2]
                atk = ("ATm", j % 2)
                tt(at, psb[pbA][:, 0:512].rearrange("p (a b) -> p a b", a=NH),
                   mask.unsqueeze(1).to_broadcast([128, NH, 128]), ALU.mult, [PS(pbA), "maskP", "maskS"], [atk])
                if is_p:
                    grp_sb = []
                    for g in range(gpt):
                        gi = j * gpt + g
                        km = KM[g % 2]
                        kmk = ("KM", g % 2)
                        ts(km, KTOK[:, j, :], GM[:, g:g + 1], ALU.mult, ["KTOK", "GMP", "GMS"], [kmk], eng=KM_ENG)
                        pbP = bank_excl("P")
                        for h in range(NH):
                            mm(psb[pbP][:, h * 128:(h + 1) * 128], km[:, h * 128:(h + 1) * 128],
                               VTOK[:, j, h * 128:(h + 1) * 128], True, True, [kmk, "VTOK"], [PS(pbP)])
                        Pv = psb[pbP][:, 0:512].rearrange("p (a b) -> p a b", a=NH)
                        elv = ELb[:, :, gi:gi + 1].to_broadcast([128, NH, 128])
                        cur = sb_ctr[0] % 3
                        nxt = (sb_ctr[0] + 1) % 3
                        sb_ctr[0] += 1
                        grp_sb.append((Sb[cur], ("Sb", cur)))
                        tt(STMP, Pv, Sst, ALU.add, [PS(pbP), "Sst"], ["STMP"])
                        tt(Sst, STMP, elv, ALU.mult, ["STMP", "ELb"], ["Sst"])
                        cp_(Sb[nxt], Sst, ["Sst"], [("Sb", nxt)], eng="act")
                    pbO = bank_excl("o")
                    for h in range(NH):
                        oh = psb[pbO][:, h * 128:(h + 1) * 128]
                        mm(oh, VTOK[:, j, h * 128:(h + 1) * 128], at[:, h, :], True, False, ["VTOK", atk], [PS(pbO)])
                        for g in range(gpt):
                            sbv, sbk = grp_sb[g]
                            mm(oh[:, g * blk:(g + 1) * blk], sbv[:, h, :], QT[:, h, c0 + g * blk:c0 + (g + 1) * blk],
                               False, g == gpt - 1, [sbk, "QT"], [PS(pbO)])
                    rms_gate(psb[pbO][:, 0:512], 512, SG[:, :, c0:c1], MIX[:, 0:NH, c0:c1], [PS(pbO)], True)
                else:
                    pbO = bank_excl("o")
                    mm(psb[pbO][:, 0:512], zeros_b, KTOK[:, j, :], True, False, ["zeros_b", "KTOK"], [PS(pbO)])
                    for h in range(NH):
                        mm(psb[pbO][:, h * 128:(h + 1) * 128], VTOK[:, j, h * 128:(h + 1) * 128], at[:, h, :], False, False,
                           ["VTOK", atk], [PS(pbO)])

                    def load_s0(g):
                        dma("sp", S0f[g % 2], sh[l, g].rearrange("h k v -> k h v"), [],
                            [("S0f", g % 2)] + (["S0GATE"] if g == 13 else []))

                    load_s0(0)
                    for g in range(gpt):
                        gi = j * gpt + g
                        s2 = g % 2
                        if g + 1 < gpt:
                            load_s0(g + 1)
                        cp_(S0b[s2], S0f[s2], [("S0f", s2)], [("S0b", s2)], eng="act")
                        for h in range(NH):
                            mm(psb[pbO][:, h * 128 + g * blk:h * 128 + (g + 1) * blk], S0b[s2][:, h, :],
                               QT[:, h, c0 + g * blk:c0 + (g + 1) * blk], False, g == gpt - 1 and h == NH - 1,
                               [("S0b", s2), "QT"], [PS(pbO)])
                        km = KM[g % 2]
                        kmk = ("KM", g % 2)
                        ts(km, KTOK[:, j, :], GM[:, g:g + 1], ALU.mult, ["KTOK", "GMP", "GMS"], [kmk], eng=KM_ENG)
                        pbP = bank_excl()
                        for h in range(NH):
                            mm(psb[pbP][:, h * 128:(h + 1) * 128], km[:, h * 128:(h + 1) * 128],
                               VTOK[:, j, h * 128:(h + 1) * 128], True, True, [kmk, "VTOK"], [PS(pbP)])
                        Pv = psb[pbP][:, 0:512].rearrange("p (a b) -> p a b", a=NH)
                        elv = ELb[:, :, gi:gi + 1].to_broadcast([128, NH, 128])
                        tt(STMP, Pv, S0f[s2], ALU.add, [PS(pbP), ("S0f", s2)], ["STMP"])
                        tt(SN[s2], STMP, elv, ALU.mult, ["STMP", "ELb"], [("SN", s2)])
                        dma("pool", hs[l, g].rearrange("h k v -> k h v"), SN[s2], [("SN", s2)], [], is_output=True)
                    rms_gate(psb[pbO][:, 0:512], 512, SG[:, :, c0:c1], MIX[:, 0:NH, c0:c1], [PS(pbO)], True)
            if is_p and tile0 + ntl == NPT:
                dma("sp", hp[l].rearrange("h k v -> k h v"), Sst, ["Sst"], [], is_output=True)

            ZH = GT[4][:, 0:n]
            ACC = GT[5][:, 0:n]
            kzh, kacc = ("GT", 4), ("GT", 5)
            for cc in range(4):
                pb = bank("cv")
                pb2 = bank("cv")
                zC = psb[pb][:, 0:n]
                zh = psb[pb][:, 256:256 + n]
                zB = psb[pb2][:, 0:n]
                for (zz, blk, pbx) in ((zC, 5, pb), (zh, 6, pb), (zB, 4, pb2)):
                    for dk in range(8):
                        mm(zz, WINB[blk][:, dk, cc * 128:(cc + 1) * 128], XTc[:, dk, 0:n], dk == 0, dk == 7,
                           ["WIN%d" % blk, "XTc"], [PS(pbx)])
                act(ZH, zh, AF.Copy, [PS(pb)], [kzh])
                CWK = ["CWT"]
                w0 = CWT[:, l, 0, cc:cc + 1]
                w1 = CWT[:, l, 1, cc:cc + 1]
                w2 = CWT[:, l, 2, cc:cc + 1]
                if is_p:
                    cp_(U[:, 0:2], CAR[:, cc, :], ["CAR"], ["U"], eng=CAR_ENG)
                    tt(U[:, 2:2 + n], zC, ZH, ALU.mult, [PS(pb), kzh, "U"], ["U"])
                    ts(ACC, U[:, 2:2 + n], w2, ALU.mult, ["U"] + CWK, [kacc])
                    stt(ACC, U[:, 1:1 + n], w1, ACC, ALU.mult, ALU.add, ["U", kacc] + CWK, [kacc])
                    stt(ACC, U[:, 0:n], w0, ACC, ALU.mult, ALU.add, ["U", kacc] + CWK, [kacc])
                    cp_(CAR[:, cc, :], U[:, n:n + 2], ["U"], ["CAR"], eng=CAR_ENG)
                else:
                    U3 = U[:, 0:160].rearrange("p (b t) -> p b t", t=10)
                    v3 = lambda ap: ap.rearrange("p (b t) -> p b t", t=8)
                    cp_(U3[:, :, 0:2], SCT[:, (l * 4 + cc) * 16:(l * 4 + cc + 1) * 16, :], [("SCT", l * 4 + cc)], ["U"])
                    tt(U3[:, :, 2:10], v3(zC), v3(ZH), ALU.mult, [PS(pb), kzh, "U"], ["U"])
                    ts(v3(ACC), U3[:, :, 2:10], w2, ALU.mult, ["U"] + CWK, [kacc])
                    stt(v3(ACC), U3[:, :, 1:9], w1, v3(ACC), ALU.mult, ALU.add, ["U", kacc] + CWK, [kacc])
                    stt(v3(ACC), U3[:, :, 0:8], w0, v3(ACC), ALU.mult, ALU.add, ["U", kacc] + CWK, [kacc])
                    cp_(CTAIL[:, cc * 16:(cc + 1) * 16, :], U3[:, :, 8:10], ["U"], ["CTAIL"])
                tt(MIX[:, NH + cc, 0:n], zB, ACC, ALU.mult, [PS(pb2), kacc], ["MIX"])
            if is_p and tile0 + ntl == NPT:
                pb = bank("xt")
                for cc in range(4):
                    tr(psb[pb][0:2, cc * 128:(cc + 1) * 128], CAR[:, cc, :], ident, ["CAR", "ident"], [PS(pb)])
                stg = GTALL[0:2, 0:512]
                cp_(stg, psb[pb][0:2, 0:512], [PS(pb)], [("GT", 0), ("GT", 1)])
                dma("sp", cp[l], stg, [("GT", 0), ("GT", 1)], [], is_output=True)
            if not is_p:
                pb = bank("xt")
                for cc in range(4):
                    tr(psb[pb][0:32, cc * 128:(cc + 1) * 128],
                       CTAIL[:, cc * 16:(cc + 1) * 16, :].rearrange("p a b -> p (a b)"), ident, ["CTAIL", "ident"],
                       [PS(pb)])
                stg = GTALL[0:32, 1024:1536]
                cp_(stg, psb[pb][0:32, 0:512], [PS(pb)], [("GT", 4), ("GT", 5)])
                dma("sp", cs[l].rearrange("b j c -> (b j) c"), stg, [("GT", 4), ("GT", 5)], [], is_output=True)

            if "mix" in taps and l == 0 and not is_p:
                dma("pool", dbg_mix, MIX[:, :, 0:128], ["MIX"], [], is_output=True)
                dma("pool", dbg_sg, SG[:, :, 0:128], ["SG"], [], is_output=True)
                dma("pool", dbg_qt, QT[:, :, 0:128], ["QT"], [], is_output=True)
                dma("pool", dbg_kt, KT[:, :, 0:128], ["KT"], [], is_output=True)
            for j in range(ntl):
                i = tile0 + j
                for half in range(2):
                    pb = bank("wo")
                    for mk in range(8):
                        mm(psb[pb][:, 0:512], MIX[:, mk, j * 128:(j + 1) * 128], WOUT[:, mk, half * 512:(half + 1) * 512],
                           mk == 0, mk == 7, ["MIX", "WOUT"], [PS(pb)])
                    Rh = R[:, i, half * 512:(half + 1) * 512]
                    stt(Rh, Rh, ALPHA, psb[pb][:, 0:512], ALU.mult, ALU.add, [("R", i), PS(pb)], [("R", i)])
                layer_norm_tile(i, l, False, affine=False)

        S.parity = None
        if ("m%d" % l) in taps:
            for i in range(NT):
                dma("sp", tap_out["m%d" % l][i], R[:, i, :], [("R", i)], [], is_output=True)

        dma("sp", LNG, ln1_g[l:l + 1, :].to_broadcast([128, D]), [], ["LNG"])
        dma("sp", LNB, ln1_b[l:l + 1, :].to_broadcast([128, D]), [], ["LNB"])
        ts(LNG, LNG, ALPHA, ALU.mult, ["LNG"], ["LNG"])
        ts(LNB, LNB, ALPHA, ALU.mult, ["LNB"], ["LNB"])

        def load_ffn(c):
            buf = c % 3
            gate = ["S0GATE"] if c < 3 else []
            dma("pool", F1[buf], w_ff1[l][:, c * 512:(c + 1) * 512].rearrange("(k p) n -> p k n", p=128), gate,
                ["F1_%d" % buf], max_dma_last_dim=8192)
            dma("pool", F2[buf], w_ff2[l][c * 512:(c + 1) * 512, :].rearrange("(k p) n -> p k n", p=128), gate,
                ["F2_%d" % buf], max_dma_last_dim=8192)

        def xt_rhs(tile0_, ntl_, dk_):
            if ntl_ == 1:
                return XTt[tile0_][:, dk_, :]
            base = XTt[tile0_]
            flat = arena[:, 0:ARENA_BYTES // 4].bitcast(BF16)
            off = xt_off[tile0_] + dk_ * 128
            return flat[:, off:off + ntl_ * 1024].rearrange("p (t r) -> p t r", r=1024)[:, :, 0:128]

        fsup = []
        t = 0
        while t < NPT:
            k = min(4, NPT - t)
            fsup.append((t, k))
            t += k
        fsup.append((NPT, 1))
        hctr = [0]
        load_ffn(0)
        load_ffn(1)
        for c in range(8):
            if c + 2 < 8:
                load_ffn(c + 2)
            if c == 1:
                dma("sp", LNG, ln2_g[l:l + 1, :].to_broadcast([128, D]), [], ["LNG"])
                dma("sp", LNB, ln2_b[l:l + 1, :].to_broadcast([128, D]), [], ["LNB"])
            buf = c % 3
            sup_c = fsup if c < 7 else [(NPT, 1)] + [(t_, 2) for t_ in range(0, NPT, 2)]
            for (tile0, ntl) in sup_c:
                n = 128 * ntl
                t0 = tile0 * 128
                if c == 0:
                    for j in range(ntl):
                        gen_xt(tile0 + j, XTt[tile0 + j], "XT%d" % (tile0 + j), affine_l=l)
                xtk = ["XT%d" % (tile0 + j) for j in range(ntl)]
                hb = HID[hctr[0] % 2]
                hbk = ("HID", hctr[0] % 2)
                hctr[0] += 1
                for fc in range(4):
                    pb = bank("f1")
                    for dk in range(8):
                        mm(psb[pb][:, 0:n], F1[buf][:, dk, fc * 128:(fc + 1) * 128], xt_rhs(tile0, ntl, dk), dk == 0,
                           dk == 7, ["F1_%d" % buf] + xtk, [PS(pb)])
                    sq = SQ[fc % 2]
                    sqk = ("SQ", fc % 2)
                    act(sq[:, 0:n], psb[pb][:, 0:n], AF.Square, [PS(pb)], [sqk])
                    stt(hb[:, fc, 0:n], psb[pb][:, 0:n], 0.0, sq[:, 0:n], ALU.is_gt, ALU.mult, [PS(pb), sqk], [hbk])
                for j in range(ntl):
                    i = tile0 + j
                    if c == 0:
                        tt(R[:, i, :], R[:, i, :], LNG, ALU.mult, [("R", i), "LNG"], [("R", i)], f=1.0)
                        tt(R[:, i, :], R[:, i, :], LNB, ALU.add, [("R", i), "LNB"], [("R", i)], f=1.0)
                    for half in range(2):
                        pb = bank("f2")
                        for fk in range(4):
                            mm(psb[pb][:, 0:512], hb[:, fk, j * 128:(j + 1) * 128],
                               F2[buf][:, fk, half * 512:(half + 1) * 512], fk == 0, fk == 3, [hbk, "F2_%d" % buf],
                               [PS(pb)])
                        Rh = R[:, i, half * 512:(half + 1) * 512]
                        tt(Rh, psb[pb][:, 0:512], Rh, ALU.add, [("R", i), PS(pb)], [("R", i)])
                    if c == 7:
                        layer_norm_tile(i, l, l == DEPTH - 1, affine_eng=LN2_ENG[i % len(LN2_ENG)])
        if ("f%d" % l) in taps:
            for i in range(NT):
                dma("sp", tap_out["f%d" % l][i], R[:, i, :], [("R", i)], [], is_output=True)

    if SCHEDULE:
        S.schedule()
        print("[kernel] simulated time (us):", S.sim_time / 1e3)
    stats = S.emit(nc, sems, dma_sems)
    print("[kernel] ops (instr, waits):", stats)
    es.close()
    return nc


_PROG_CACHE = {}


def _get_prog(n_ptiles=16, taps=()):
    key = (n_ptiles, tuple(taps))
    if key not in _PROG_CACHE:
        _PROG_CACHE[key] = build_program(n_ptiles, taps)
    return _PROG_CACHE[key]


def _in_maps(inp, n_ptiles=16):
    seql = n_ptiles * 128
    f = lambda a: np.ascontiguousarray(np.asarray(a, dtype=np.float32))
    shared = {k: f(inp[k]) for k in ("w_in", "lb_logits", "conv_w", "onorm_g", "w_out", "ln1_g", "ln1_b", "w_ff1",
                                     "w_ff2", "ln2_g", "ln2_b")}
    maps = []
    for c in range(N_CORES):
        m = dict(shared)
        m["xp"] = f(inp["x_prompt"][c, :seql])
        m["xs"] = f(np.asarray(inp["x_sample"])[16 * c:16 * (c + 1)].reshape(128, D))
        m["sh"] = f(np.asarray(inp["state_hgrn"])[:, 16 * c:16 * (c + 1)])
        m["sc"] = f(np.asarray(inp["state_conv"])[:, 16 * c:16 * (c + 1)])
        maps.append(m)
    return maps


def kernel(**inputs):
    nc = _get_prog()
    res = run_bass_kernel_spmd(nc, _in_maps(inputs), core_ids=list(range(N_CORES)))
    r = res.results
    y_prompt = np.stack([r[c]["yp"] for c in range(N_CORES)], axis=0).astype(np.float32)
    y_sample = np.concatenate([r[c]["ys"].reshape(16, DEC_T, D) for c in range(N_CORES)], axis=0).astype(np.float32)
    new_hgrn_prompt = np.stack([r[c]["hp"] for c in range(N_CORES)], axis=1).astype(np.float32)
    new_conv_prompt = np.stack([r[c]["cp"] for c in range(N_CORES)], axis=1).astype(np.float32)
    new_hgrn_sample = np.concatenate([r[c]["hs"] for c in range(N_CORES)], axis=1).astype(np.float32)
    new_conv_sample = np.concatenate([r[c]["cs"] for c in range(N_CORES)], axis=1).astype(np.float32)
    return (y_prompt, y_sample, new_hgrn_prompt, new_conv_prompt, new_hgrn_sample, new_conv_sample)
```

```python
from contextlib import ExitStack

import numpy as np
import concourse.bass as bass
import concourse.mybir as mybir
from concourse.bass_utils import run_bass_kernel_spmd

F32 = mybir.dt.float32
BF16 = mybir.dt.bfloat16
AF = mybir.ActivationFunctionType
ALU = mybir.AluOpType

N_CORES = 8
D = 1024
DEPTH = 2
SEQ = 2048
DEC_B = 128
DEC_T = 8
HW = 512
CW_ = 512
NH = 4
PW = 3584
DFF = 4096
ALPHA = float((2 * DEPTH) ** 0.25)
LN_EPS = 1e-5
RMS_EPS = 1e-6
N_DMA_SEMS = 16
SCHEDULE = True
import os as _os
AHEAD_RESERVE = 0
SYNC_LAT = float(_os.environ.get('SYNC_LAT', '250'))
import os as _os
LN2_ENG = tuple(_os.environ.get('LN2_ENG', 'dve,pool').split(','))
KTOK_ENG = _os.environ.get('KTOK_ENG', 'act')
KM_ENG = _os.environ.get('KM_ENG', 'dve')
CAR_ENG = _os.environ.get('CAR_ENG', 'dve')
LN1_ENG = tuple(_os.environ.get('LN1_ENG', 'pool').split(','))
SIM_NOFENCE = bool(_os.environ.get('SIM_NOFENCE'))
SIM_DB = _os.environ.get('SIM_DB', '')


class LazyAP:
    phys = None
    bankmap = None

    def __init__(self, tid, ops=()):
        self.tid = tid
        self.ops = ops

    def __getitem__(self, key):
        return LazyAP(self.tid, self.ops + (("g", key),))

    def rearrange(self, s, **kw):
        return LazyAP(self.tid, self.ops + (("r", s, kw),))

    def bitcast(self, dt):
        return LazyAP(self.tid, self.ops + (("b", dt),))

    def _apply(self, t):
        for o in self.ops:
            if o[0] == "g":
                t = t[o[1]]
            elif o[0] == "r":
                t = t.rearrange(o[1], **o[2])
            else:
                t = t.bitcast(o[1])
        return t

    @property
    def shape(self):
        return self._apply(LazyAP.phys[0]).shape

    def resolve(self):
        return self._apply(LazyAP.phys[LazyAP.bankmap[self.tid]])


class _LazyBanks:
    def __getitem__(self, tid):
        return LazyAP(tid)


def rs(x):
    return x.resolve() if isinstance(x, LazyAP) else x


class _Op:
    __slots__ = ("eng", "fn", "dma", "idx", "deps", "sig", "semval", "dsem", "dval", "prev_dval", "deps_all",
                 "cost", "nbytes", "desc", "cost_start", "cost_fin")

    def __init__(self, eng, fn, dma, idx):
        self.eng = eng
        self.fn = fn
        self.dma = dma
        self.idx = idx
        self.deps = ()
        self.deps_all = ()
        self.cost = 100.0
        self.nbytes = 0
        self.sig = False
        self.semval = 0
        self.dsem = None
        self.dval = 0
        self.prev_dval = 0


class Sched:
    ENGS = ("pe", "act", "dve", "pool", "sp")

    def __init__(self):
        self.ops = []
        self.last_w = {}
        self.readers = {}
        self.out_dmas = []
        self.ranges = {}
        self.alias = None
        self.subkeys = {}
        self.epoch = {}
        self.absorbed = {}

    def reg(self, name, lo, hi):
        if name in self.ranges:
            a, b = self.ranges[name]
            lo, hi = min(a, lo), max(b, hi)
        self.ranges[name] = (lo, hi)

    def _aliases(self, key):
        name = key if isinstance(key, str) else key[0]
        if self.alias is None:
            self.alias = {}
            items = list(self.ranges.items())
            for n, (lo, hi) in items:
                self.alias[n] = [m for m, (a, b) in items if m != n and a < hi and lo < b]
        self.subkeys.setdefault(name, set()).add(key)
        out = []
        for m in self.alias.get(name, ()):
            out.extend(self.subkeys.get(m, ()))
        return out

    def add(self, eng, fn, reads=(), writes=(), dma=False, is_output=False, cost=100.0, nbytes=0):
        if SIM_DB and getattr(self, "parity", None) is not None:
            ren = lambda k: (k, self.parity) if (k if isinstance(k, str) else k[0]) in SIM_DB.split(",") else k
            reads = [ren(k) for k in reads]
            writes = [ren(k) for k in writes]
        idx = len(self.ops)
        psr = [k for k in reads if isinstance(k, tuple) and k[0] == "ps"]
        if psr:
            writes = list(writes) + [k for k in psr if k not in writes]
        op = _Op(eng, fn, dma, idx)
        op.cost = cost
        op.nbytes = nbytes
        op.desc = (tuple(reads), tuple(writes))
        deps = set()
        for k in reads:
            self._aliases(k)
            w = self.last_w.get(k)
            if w is not None:
                deps.add(w)
        for k in writes:
            w = self.last_w.get(k)
            if w is not None:
                deps.add(w)
            for r in self.readers.get(k, ()):
                deps.add(r)
            for k2 in self._aliases(k):
                ep = self.epoch.get(k2, 0)
                if self.absorbed.get((k, k2)) == ep:
                    continue
                self.absorbed[(k, k2)] = ep
                w = self.last_w.get(k2)
                if w is not None:
                    deps.add(w)
                for r in self.readers.get(k2, ()):
                    deps.add(r)
        deps.discard(idx)
        for k in reads:
            self.readers.setdefault(k, []).append(idx)
            self.epoch[k] = self.epoch.get(k, 0) + 1
        for k in writes:
            self.last_w[k] = idx
            self.readers[k] = []
            self.epoch[k] = self.epoch.get(k, 0) + 1
        op.deps_all = tuple(sorted(deps))
        op.deps = tuple(d for d in op.deps_all
                        if not (eng == "pe" and not dma and self.ops[d].eng == "pe" and not self.ops[d].dma))
        self.ops.append(op)
        if is_output:
            self.out_dmas.append(idx)
        return idx

    def schedule(self, window=48, sync_lat=SYNC_LAT, dma_lat=2200.0, dma_bw=180.0):
        ops = self.ops
        n = len(ops)
        succ = [[] for _ in range(n)]
        cnt = [0] * n
        for op in ops:
            cnt[op.idx] = len(op.deps_all)
            for d in op.deps_all:
                succ[d].append(op.idx)
        rtime = [0.0] * n
        crit_dep = {}
        op_prev_on_eng = {}
        last_on_eng = {}
        self.crit_dep = crit_dep
        self.op_prev_on_eng = op_prev_on_eng
        tile_ops = {}
        for op in ops:
            for k in set(op.desc[0]) | set(op.desc[1]):
                if isinstance(k, tuple) and k[0] == "ps":
                    tile_ops.setdefault(k[1], []).append(op.idx)
        alloc_order = sorted(tile_ops, key=lambda t: tile_ops[t][0])
        first_of = {tile_ops[t][0]: t for t in alloc_order}
        op_tiles = {}
        for t, lst in tile_ops.items():
            for i in lst:
                op_tiles.setdefault(i, []).append(t)
        next_alloc = [0]
        NB = 8
        ev = []
        for t, lst in tile_ops.items():
            ev.append((lst[0], 1))
            ev.append((lst[-1] + 0.5, -1))
        live = mx = 0
        for _, d in sorted(ev):
            live += d
            mx = max(mx, live)
        ahead_max = max(0, NB - mx - AHEAD_RESERVE)
        self.max_live = mx
        rank = {t: i for i, t in enumerate(alloc_order)}
        allocated = set()
        tenant = [None] * NB
        bank_free = [0.0] * NB
        remaining = {t: len(lst) for t, lst in tile_ops.items()}
        tile_fin = {t: 0.0 for t in tile_ops}
        bankmap = {}
        pend = {e: [] for e in self.ENGS}
        for op in ops:
            pend[op.eng].append(op.idx)
        head = {e: 0 for e in self.ENGS}
        done = {e: [False] * len(pend[e]) for e in self.ENGS}
        etime = {e: 0.0 for e in self.ENGS}
        dma_free = [0.0]
        order = {e: [] for e in self.ENGS}
        left = n
        tmax = 0.0
        while left:
            best = None
            for e in self.ENGS:
                lst = pend[e]
                dn = done[e]
                h = head[e]
                while h < len(lst) and dn[h]:
                    h += 1
                head[e] = h
                seen = 0
                k = h
                et = etime[e]
                while k < len(lst) and seen < window:
                    if not dn[k]:
                        seen += 1
                        i = lst[k]
                        if cnt[i] == 0:
                            rdy = rtime[i] if rtime[i] > et else et
                            t_new = first_of.get(i)
                            if t_new is not None:
                                if alloc_order[next_alloc[0]] != t_new:
                                    n_ahead = sum(1 for b in range(NB) if tenant[b] is not None
                                                  and remaining[tenant[b]] > 0 and rank[tenant[b]] > next_alloc[0])
                                    if n_ahead >= ahead_max:
                                        k += 1
                                        continue
                                fb = None
                                for b in range(NB):
                                    if tenant[b] is None or remaining[tenant[b]] == 0:
                                        if fb is None or bank_free[b] < bank_free[fb]:
                                            fb = b
                                if fb is None:
                                    k += 1
                                    continue
                                if bank_free[fb] + sync_lat > rdy:
                                    rdy = bank_free[fb] + sync_lat
                            cand = (rdy + 0.02 * (k - h), i, e, k, rdy)
                            if best is None or cand < best:
                                best = cand
                    k += 1
            assert best is not None, "scheduler deadlock"
            _, idx, e, k, rdy = best
            op = ops[idx]
            t_new = first_of.get(idx)
            if t_new is not None:
                fb = None
                for b in range(NB):
                    if tenant[b] is None or remaining[tenant[b]] == 0:
                        if fb is None or bank_free[b] < bank_free[fb]:
                            fb = b
                prev = tenant[fb]
                if prev is not None:
                    extra = tuple(tile_ops[prev])
                    op.deps_all = tuple(sorted(set(op.deps_all) | set(extra)))
                    op.deps = tuple(d for d in op.deps_all
                                    if not (op.eng == "pe" and not op.dma and ops[d].eng == "pe" and not ops[d].dma))
                tenant[fb] = t_new
                bankmap[t_new] = fb
                allocated.add(t_new)
                while next_alloc[0] < len(alloc_order) and alloc_order[next_alloc[0]] in allocated:
                    next_alloc[0] += 1
            if op.dma:
                if op.nbytes <= 600000:
                    f = rdy + dma_lat + 2.0 * op.nbytes / dma_bw
                else:
                    st = max(rdy, dma_free[0])
                    dma_free[0] = st + op.nbytes / dma_bw
                    f = dma_free[0] + dma_lat
                etime[e] = rdy + 60.0
            else:
                f = rdy + op.cost
                etime[e] = f
            op.cost_start = rdy
            op.cost_fin = f
            if f > tmax:
                tmax = f
            for t in op_tiles.get(idx, ()):
                remaining[t] -= 1
                if f > tile_fin[t]:
                    tile_fin[t] = f
                if remaining[t] == 0:
                    bank_free[bankmap[t]] = tile_fin[t]
            for s in succ[idx]:
                cnt[s] -= 1
                if f + sync_lat > rtime[s]:
                    rtime[s] = f + sync_lat
                    crit_dep[s] = idx
            op_prev_on_eng[idx] = last_on_eng.get(e)
            last_on_eng[e] = idx
            done[e][k] = True
            order[e].append(idx)
            left -= 1
        self.order = order
        self.sim_time = tmax
        LazyAP.bankmap = bankmap
        return order

    def emit(self, nc, sems, dma_sems):
        ops = self.ops
        order = getattr(self, "order", None)
        if order is None:
            order = {e: [op.idx for op in ops if op.eng == e] for e in self.ENGS}
        for op in ops:
            for d in op.deps:
                ops[d].sig = True
        cnt = {e: 0 for e in self.ENGS}
        dcnt = {e: 0 for e in self.ENGS}
        for op in (ops[i] for e in self.ENGS for i in order[e]):
            if op.dma:
                j = dcnt[op.eng]
                dcnt[op.eng] += 1
                pool = dma_sems[op.eng]
                op.dsem = pool[j % len(pool)]
                op.dval = 16 * (j // len(pool) + 1)
                op.prev_dval = 16 * (j // len(pool))
            elif op.sig:
                cnt[op.eng] += 1
                op.semval = cnt[op.eng]
        stats = {e: [0, 0] for e in self.ENGS}

        def run(ename, e):
            seen = {}

            def wait(sem, val):
                key = id(sem)
                if seen.get(key, 0) >= val:
                    return
                e.wait_ge(sem, val)
                seen[key] = val
                stats[ename][1] += 1

            for op in (ops[i] for i in order[ename]):
                need = {}
                for d in op.deps:
                    dop = ops[d]
                    sm, vv = (dop.dsem, dop.dval) if dop.dma else (sems[dop.eng], dop.semval)
                    if need.get(id(sm), (None, 0))[1] < vv:
                        need[id(sm)] = (sm, vv)
                for sm, vv in need.values():
                    wait(sm, vv)
                if op.dma:
                    if op.prev_dval > 0:
                        wait(op.dsem, op.prev_dval)
                    ins = op.fn(e)
                    ins.then_inc(op.dsem, 16)
                else:
                    ins = op.fn(e)
                    if op.sig:
                        ins.then_inc(sems[op.eng], 1)
                stats[ename][0] += 1
            if ename == "sp":
                for d in self.out_dmas:
                    dop = ops[d]
                    wait(dop.dsem, dop.dval)

        with nc.Block() as block:
            @block.tensor
            def _(e):
                run("pe", e)

            @block.scalar
            def _(e):
                run("act", e)

            @block.vector
            def _(e):
                run("dve", e)

            @block.gpsimd
            def _(e):
                run("pool", e)

            @block.sync
            def _(e):
                run("sp", e)
        return stats


def build_program(n_ptiles=16, taps=()):
    assert n_ptiles % 2 == 0
    NPT = n_ptiles
    NT = NPT + 1
    SEQL = NPT * 128
    nc = bass.Bass("TRN2", target_bir_lowering=False)

    def din(name, shape):
        return nc.dram_tensor(name, list(shape), F32, kind="ExternalInput").ap()

    def dout(name, shape):
        return nc.dram_tensor(name, list(shape), F32, kind="ExternalOutput").ap()

    xp = din("xp", [SEQL, D])
    xs = din("xs", [128, D])
    sh = din("sh", [DEPTH, 16, NH, 128, 128])
    sc = din("sc", [DEPTH, 16, 2, CW_])
    w_in = din("w_in", [DEPTH, D, PW])
    lb_logits = din("lb_logits", [DEPTH, HW])
    conv_w = din("conv_w", [DEPTH, 3, CW_])
    onorm_g = din("onorm_g", [DEPTH, HW])
    w_out = din("w_out", [DEPTH, D, D])
    ln1_g = din("ln1_g", [DEPTH, D])
    ln1_b = din("ln1_b", [DEPTH, D])
    w_ff1 = din("w_ff1", [DEPTH, D, DFF])
    w_ff2 = din("w_ff2", [DEPTH, DFF, D])
    ln2_g = din("ln2_g", [DEPTH, D])
    ln2_b = din("ln2_b", [DEPTH, D])

    yp = dout("yp", [SEQL, D])
    ys = dout("ys", [128, D])
    hp = dout("hp", [DEPTH, NH, 128, 128])
    cp = dout("cp", [DEPTH, 2, CW_])
    hs = dout("hs", [DEPTH, 16, NH, 128, 128])
    cs = dout("cs", [DEPTH, 16, 2, CW_])
    tap_out = {t: dout("tap_" + t, [NT, 128, D]) for t in taps if t != "mix"}
    if "mix" in taps:
        dbg_mix = dout("dbg_mix", [128, 8, 128])
        dbg_sg = dout("dbg_sg", [128, 4, 128])
        dbg_qt = dout("dbg_qt", [128, 4, 128])
        dbg_kt = dout("dbg_kt", [128, 4, 128])

    es = ExitStack()
    ARENA_BYTES = 211968
    arena = es.enter_context(nc.sbuf_tensor("arena", [128, ARENA_BYTES // 4], F32))
    LazyAP.phys = [es.enter_context(nc.psum_tensor(f"ps{i}", [128, 512], F32)) for i in range(8)]
    psb = _LazyBanks()
    sems = {e: es.enter_context(nc.semaphore(f"s_{e}")) for e in Sched.ENGS}
    dma_sems = {e: [es.enter_context(nc.semaphore(f"d_{e}{i}")) for i in range(N_DMA_SEMS if e == "sp" else 6)]
                for e in ("sp", "pool")}
    for e in ("act", "pe", "dve"):
        dma_sems[e] = dma_sems["sp"]

    class Arena:
        def __init__(self, base, limit):
            self.off = base
            self.limit = limit

        def alloc(self, shape, dt, name=None):
            nel = int(np.prod(shape[1:]))
            nbytes = nel * (4 if dt == F32 else 2)
            nbytes = (nbytes + 31) // 32 * 32
            o = self.off
            self.off += nbytes
            assert self.off <= self.limit, (self.off, self.limit, shape)
            if name is not None:
                S.reg(name, o, o + nbytes)
            v = arena[:, o // 4:(o + nbytes) // 4]
            if dt != F32:
                v = v.bitcast(dt)
            v = v[:, 0:nel]
            if len(shape) == 3:
                v = v.rearrange("p (a b) -> p a b", a=shape[1])
            elif len(shape) == 4:
                v = v.rearrange("p (a b c) -> p a b c", a=shape[1], b=shape[2])
            return v

    S = Sched()
    A0 = Arena(0, ARENA_BYTES)
    R = A0.alloc([128, NT, D], F32)
    ident = A0.alloc([128, 128], F32)
    identb = A0.alloc([128, 128], BF16)
    ones_s = A0.alloc([128, 128], BF16)
    zeros_b = A0.alloc([128, 128], BF16)
    maskP = A0.alloc([128, 128], F32)
    maskS = A0.alloc([128, 128], F32)
    RMP = A0.alloc([128, 256], F32)
    RMS = A0.alloc([128, 128], F32)
    GMP = A0.alloc([128, 2], F32)
    GMS = A0.alloc([128, 16], F32)
    LBR = A0.alloc([128, DEPTH, NH], F32)
    LB = A0.alloc([128, DEPTH, NH], F32)
    OML = A0.alloc([128, DEPTH, NH], F32)
    LNOML = A0.alloc([128, DEPTH, NH], F32)
    LBT = A0.alloc([128, NH], F32)
    CWT = A0.alloc([128, DEPTH, 3, 4], F32)
    ONG = A0.alloc([128, DEPTH, NH], F32)
    SCT = A0.alloc([128, DEPTH * 4 * 16, 2], F32)
    CTAIL = A0.alloc([128, 4 * 16, 2], F32)
    CAR = A0.alloc([128, 4, 2], F32)
    FSCR = A0.alloc([128, 8], F32)
    G1c = A0.alloc([128, DEPTH, 8], F32)
    B1c = A0.alloc([128, DEPTH, 8], F32)
    LNG = A0.alloc([128, D], F32)
    LNB = A0.alloc([128, D], F32)
    LNS = [dict(ST=A0.alloc([128, 12], F32), MV=A0.alloc([128, 2], F32), LNV=A0.alloc([128, 1], F32),
                RSTD=A0.alloc([128, 1], F32), NMR=A0.alloc([128, 1], F32)) for _ in range(4)]
    persist_end = A0.off
    XT_BYTES = 8 * NT * 128 * 2
    P0 = persist_end
    NS = 256
    AM = Arena(P0, ARENA_BYTES)
    WINB = [AM.alloc([128, 8, 512], BF16, "WIN%d" % b) for b in range(7)]
    WOUT = AM.alloc([128, 8, D], BF16, "WOUT")
    XTc = AM.alloc([128, 8, NS], BF16, "XTc")
    QT = AM.alloc([128, NH, NS], BF16, "QT")
    KT = AM.alloc([128, NH, NS], BF16, "KT")
    KTOK = AM.alloc([128, 2, 512], BF16, "KTOK")
    VTOK = AM.alloc([128, 2, 512], BF16, "VTOK")
    GTALL = AM.alloc([128, 6 * NS], F32, "GT")
    GT = [GTALL[:, i * NS:(i + 1) * NS] for i in range(6)]
    RST = GTALL[:, 0:512]
    TMPO = GTALL[:, 512:1024]
    SG = AM.alloc([128, NH, NS], F32, "SG")
    ELb = AM.alloc([128, NH, 16], F32, "ELb")
    early_end = AM.off
    ATm = [AM.alloc([128, NH, 128], BF16, "ATm") for _ in range(2)]
    KM = [AM.alloc([128, 512], BF16, "KM") for _ in range(2)]
    Sst = AM.alloc([128, NH, 128], F32, "Sst")
    STMP = AM.alloc([128, NH, 128], F32, "STMP")
    Sb = [AM.alloc([128, NH, 128], BF16, "Sb") for _ in range(3)]
    S0f = [AM.alloc([128, NH, 128], F32, "S0f") for _ in range(4)]
    S0b = [AM.alloc([128, NH, 128], BF16, "S0b") for _ in range(2)]
    late0 = AM.off
    OSQ = AM.alloc([128, 512], BF16, "OSQ")
    MIX = AM.alloc([128, 8, NS], BF16, "MIX")
    U = AM.alloc([128, NS + 8], F32, "U")
    mixer_end = AM.off
    AF_ = Arena(P0, ARENA_BYTES)
    F1 = [None] * 3
    F2 = [None] * 3
    F1[0] = AF_.alloc([128, 8, 512], BF16, "F1_0")
    F2[0] = AF_.alloc([128, 4, D], BF16, "F2_0")
    XTt = [AF_.alloc([128, 8, 128], BF16, "XT%d" % i_) for i_ in range(NT)]
    xt_off = [(P0 + 2 * 8192 + i_ * 2048) // 2 for i_ in range(NT)]
    SQ = [AF_.alloc([128, 512], F32, "SQ") for _ in range(2)]
    assert AF_.off <= P0 + 7 * 8192, AF_.off
    AF_.off = P0 + 7 * 8192
    F1[1] = AF_.alloc([128, 8, 512], BF16, "F1_1")
    F2[1] = AF_.alloc([128, 4, D], BF16, "F2_1")
    F1[2] = AF_.alloc([128, 8, 512], BF16, "F1_2")
    F2[2] = AF_.alloc([128, 4, D], BF16, "F2_2")
    assert AF_.off <= early_end, (AF_.off, early_end)
    AF_.off = late0
    HID = [AF_.alloc([128, 4, 512], BF16, "HID") for _ in range(2)]
    ffn_end = AF_.off
    print(f"[kernel] arena: persist={persist_end} mixer_end={mixer_end} ffn_end={ffn_end} limit={ARENA_BYTES}")

    bank_ctr = [0]

    SIM_INFBANK = bool(_os.environ.get("SIM_INFBANK"))
    uniq = [0]

    cur_uid = {}

    def bank(cls=None):
        uniq[0] += 1
        return uniq[0]

    def fsz(ap):
        return int(np.prod(ap.shape[1:]))

    def mm(out, lhsT, rhs, start, stop, reads, writes):
        S.add("pe", lambda e: e.matmul(rs(out), lhsT=lhsT, rhs=rhs, start=start, stop=stop), reads, writes,
              cost=max(64, fsz(rhs)) / 2.4 + 8.0)

    def tr(out, in_, idn, reads, writes):
        S.add("pe", lambda e: e.transpose(rs(out), in_, idn), reads, writes, cost=110.0)

    def act(out, in_, func, reads, writes, scale=None, bias=None):
        kw = {}
        if scale is not None:
            kw["scale"] = scale
        if bias is not None:
            kw["bias"] = bias
        S.add("act", lambda e: e.activation(out=out, in_=rs(in_), func=func, **kw), reads, writes,
              cost=220.0 + 0.83 * fsz(in_))

    def ecost(eng, n, f):
        return (100.0 + 1.04 * f * n) * (2.0 if eng == "pool" else 1.0)

    def tt(out, a, b, op, reads, writes, eng="dve", f=1.5):
        S.add(eng, lambda e: e.tensor_tensor(out=out, in0=rs(a), in1=rs(b), op=op), reads, writes,
              cost=ecost(eng, fsz(out), f))

    def ts(out, in0, s1, op0, reads, writes, s2=None, op1=None, eng="dve"):
        c = ecost(eng, fsz(out), 1.0)
        if op1 is None:
            S.add(eng, lambda e: e.tensor_scalar(out=out, in0=rs(in0), scalar1=s1, scalar2=None, op0=op0), reads, writes,
                  cost=c)
        else:
            S.add(eng, lambda e: e.tensor_scalar(out=out, in0=rs(in0), scalar1=s1, scalar2=s2, op0=op0, op1=op1),
                  reads, writes, cost=c)

    def stt(out, in0, scalar, in1, op0, op1, reads, writes):
        S.add("dve", lambda e: e.scalar_tensor_tensor(out=out, in0=rs(in0), scalar=scalar, in1=rs(in1), op0=op0,
                                                        op1=op1),
              reads, writes, cost=ecost("dve", fsz(out), 1.3))

    def cp_(out, in_, reads, writes, eng="dve"):
        if eng == "act":
            S.add("act", lambda e: e.copy(out=out, in_=rs(in_)), reads, writes, cost=220.0 + 0.83 * fsz(out))
        else:
            S.add(eng, lambda e: e.tensor_copy(out=out, in_=rs(in_)), reads, writes, cost=ecost(eng, fsz(out), 1.0))

    def memset(ap, val, reads, writes, eng="pool"):
        S.add(eng, lambda e: e.memset(ap, val), reads, writes, cost=ecost(eng, fsz(ap), 1.0))

    def dma(eng, out, in_, reads, writes, is_output=False, **kw):
        nb = 128 * fsz(out) * 4
        S.add(eng, lambda e: e.dma_start(out=out, in_=in_, **kw), reads, writes, dma=True, is_output=is_output,
              nbytes=nb)

    def asel(ap, pattern, cm, base, reads, writes):
        S.add("pool", lambda e: e.affine_select(out=ap, in_=ap, pattern=pattern, compare_op=ALU.is_ge, fill=0.0,
                                                 base=base, channel_multiplier=cm), reads, writes, cost=400.0)

    PS = lambda b: ("ps", b)

    memset(ident, 1.0, [], ["ident"])
    S.add("pool", lambda e: e.affine_select(out=ident, in_=ident, pattern=[[-1, 128]], compare_op=ALU.is_equal,
                                             fill=0.0, base=0, channel_multiplier=1), ["ident"], ["ident"])
    cp_(identb, ident, ["ident"], ["identb"], eng="pool")
    memset(ones_s, 1.0 / 128.0, [], ["ones_s"])
    memset(zeros_b, 0.0, [], ["zeros_b"])
    for (mk, blk, nm) in ((maskP, 64, "maskP"), (maskS, 8, "maskS")):
        nb = 128 // blk
        memset(mk, 1.0, [], [nm])
        v3 = mk.rearrange("p (a b) -> p a b", b=blk)
        asel(v3, [[blk, nb], [1, blk]], -1, 0, [nm], [nm])
        asel(v3, [[-blk, nb], [0, blk]], 1, 0, [nm], [nm])
    memset(RMP, 1.0, [], ["RMP"])
    memset(RMP.rearrange("p (a b) -> p a b", b=64)[:, :, 0:1], 0.0, ["RMP"], ["RMP"])
    memset(RMS, 1.0, [], ["RMS"])
    memset(RMS.rearrange("p (a b) -> p a b", b=8)[:, :, 0:1], 0.0, ["RMS"], ["RMS"])
    for (gm, blk, nm) in ((GMP, 64, "GMP"), (GMS, 8, "GMS")):
        nb = 128 // blk
        memset(gm, 1.0, [], [nm])
        asel(gm, [[-blk, nb]], 1, 0, [nm], [nm])
        asel(gm, [[blk, nb]], -1, blk - 1, [nm], [nm])
    STG = LNG
    STGK = [("STGR", i_) for i_ in range(6)]
    memset(STG[0:96, :], 0.0, [], ["LNG"] + STGK, eng="dve")
    dma("sp", STG[0:2, 0:512], lb_logits, [], [STGK[0]])
    dma("sp", STG[2:4, 0:512], onorm_g, [], [STGK[1]])
    dma("sp", STG[4:10, 0:512], conv_w.rearrange("l j c -> (l j) c"), [], [STGK[2]])
    dma("sp", STG[10:12, :], ln1_g, [], [STGK[3]])
    dma("sp", STG[12:14, :], ln1_b, [], [STGK[4]])
    dma("sp", STG[16:80, 0:512], sc.rearrange("l b j c -> (l b j) c"), [], [STGK[5]])
    for k in range(8):
        pb = bank("xt")
        pv = psb[pb][:, 0:80]
        tr(pv, STG[0:80, k * 128:(k + 1) * 128], ident[0:80, 0:80], ["LNG", "ident"] + STGK, [PS(pb)])
        cp_(G1c[:, :, k], psb[pb][:, 10:12], [PS(pb)], ["G1c"])
        cp_(B1c[:, :, k], psb[pb][:, 12:14], [PS(pb)], ["B1c"])
        if k < 4:
            cp_(LBR[:, :, k], psb[pb][:, 0:2], [PS(pb)], ["LBR"])
            cp_(ONG[:, :, k], psb[pb][:, 2:4], [PS(pb)], ["ONG"])
            cp_(CWT[:, :, :, k], psb[pb][:, 4:10].rearrange("p (a b) -> p a b", a=DEPTH), [PS(pb)], ["CWT"])
            for l_ in range(DEPTH):
                cp_(SCT[:, (l_ * 4 + k) * 16:(l_ * 4 + k + 1) * 16, :],
                    psb[pb][:, 16 + l_ * 32:16 + (l_ + 1) * 32].rearrange("p (a b) -> p a b", a=16), [PS(pb)],
                    [("SCT", l_ * 4 + k)], eng="act")
    memset(LB[:, 0, :], 0.0, [], ["LB"], eng="dve")
    memset(OML[:, 0, :], 1.0, [], ["OML"], eng="dve")
    memset(LNOML[:, 0, :], 0.0, [], ["LNOML"], eng="dve")
    tt(LBT, LBR[:, 0, :], LBR[:, 1, :], ALU.subtract, ["LBR"], ["LBT"])
    act(LBR[:, 0, :], LBT, AF.Exp, ["LBT"], ["LBR"])
    act(LBR[:, 0, :], LBR[:, 0, :], AF.Ln, ["LBR"], ["LBR"], bias=1.0)
    act(LB[:, 1, :], LBR[:, 0, :], AF.Exp, ["LBR", "LB"], ["LB"], scale=-1.0)
    tt(LNOML[:, 1, :], LBT, LBR[:, 0, :], ALU.subtract, ["LBT", "LBR", "LNOML"], ["LNOML"])
    act(OML[:, 1, :], LNOML[:, 1, :], AF.Exp, ["LNOML", "OML"], ["OML"])

    for i in range(2):
        dma("sp", R[:, i, :], xp[i * 128:(i + 1) * 128, :], [], [("R", i)])

    def late_input_loads():
        for i in range(2, NPT):
            dma("sp", R[:, i, :], xp[i * 128:(i + 1) * 128, :], ["WIN4"], [("R", i)])
        dma("sp", R[:, NPT, :], xs, ["WIN4"], [("R", NPT)])

    ln_ctr = [0]

    def layer_norm_tile(i, l, final_out, affine_eng="pool", affine=True):
        k = ln_ctr[0] % 4
        ln_ctr[0] += 1
        b = LNS[k]
        lk = ("LNS", k)
        rk = ("R", i)
        Ri = R[:, i, :]
        S.add("dve", lambda e: e.bn_stats(out=b["ST"][:, 0:6], in_=Ri[:, 0:512]), [rk], [lk], cost=640.0)
        S.add("dve", lambda e: e.bn_stats(out=b["ST"][:, 6:12], in_=Ri[:, 512:1024]), [rk, lk], [lk], cost=640.0)
        S.add("dve", lambda e: e.bn_aggr(out=b["MV"], in_=b["ST"]), [lk], [lk], cost=200.0)
        act(b["LNV"], b["MV"][:, 1:2], AF.Ln, [lk], [lk], bias=LN_EPS)
        act(b["RSTD"], b["LNV"], AF.Exp, [lk], [lk], scale=-0.5)
        stt(b["NMR"], b["MV"][:, 0:1], -1.0, b["RSTD"], ALU.mult, ALU.mult, [lk], [lk])
        act(Ri, Ri, AF.Identity, [rk, lk], [rk], scale=b["RSTD"], bias=b["NMR"])
        if affine:
            tt(Ri, Ri, LNG, ALU.mult, [rk, "LNG"], [rk], eng=affine_eng, f=1.0)
            tt(Ri, Ri, LNB, ALU.add, [rk, "LNB"], [rk], eng=affine_eng, f=1.0)
        if final_out:
            if i < NPT:
                dma("sp", yp[i * 128:(i + 1) * 128, :], Ri, [rk], [], is_output=True)
            else:
                dma("sp", ys, Ri, [rk], [], is_output=True)

    def gen_xt(i, dst, dst_key, affine_l=None):
        for half in range(2):
            pb = bank("xt")
            for q in range(4):
                dk = half * 4 + q
                tr(psb[pb][:, q * 128:(q + 1) * 128], R[:, i, dk * 128:(dk + 1) * 128], ident,
                   [("R", i), "ident"], [PS(pb)])
            if affine_l is None:
                act(dst[:, half * 4:(half + 1) * 4, :], psb[pb][:, 0:512].rearrange("p (a b) -> p a b", a=4), AF.Copy,
                    [PS(pb)], [dst_key])
            else:
                for q in range(4):
                    dk = half * 4 + q
                    act(dst[:, dk, :], psb[pb][:, q * 128:(q + 1) * 128], AF.Identity, [PS(pb), "G1c", "B1c"], [dst_key],
                        scale=G1c[:, affine_l, dk:dk + 1], bias=B1c[:, affine_l, dk:dk + 1])

    for l in range(DEPTH):
        for blk in (1, 0, 3, 2, 5, 6, 4):
            dma("pool", WINB[blk], w_in[l][:, blk * 512:(blk + 1) * 512].rearrange("(k p) n -> p k n", p=128), [],
                ["WIN%d" % blk], max_dma_last_dim=8192)
        dma("pool", WOUT, w_out[l].rearrange("(k p) n -> p k n", p=128), [], ["WOUT"], max_dma_last_dim=8192)
        if l == 0:
            late_input_loads()
        memset(Sst, 0.0, [], ["Sst"], eng="dve")
        memset(Sb[0], 0.0, [], [("Sb", 0)], eng="dve")
        memset(CAR, 0.0, [], ["CAR"], eng="dve")
        sb_ctr = [0]

        supers = [("p", 2 * i, 2) for i in range(NPT // 2)] + [("s", NPT, 1)]
        for (kind, tile0, ntl) in supers:
            S.parity = (tile0 // 2) % 2
            n = 128 * ntl
            is_p = kind == "p"
            blk = 64 if is_p else 8
            G = n // blk
            gpt = 128 // blk
            RM = RMP if is_p else RMS
            GM = GMP if is_p else GMS
            mask = maskP if is_p else maskS
            for j in range(ntl):
                gen_xt(tile0 + j, XTc[:, :, j * 128:(j + 1) * 128], "XTc")
            for h in range(NH):
                tset = (h % 2) * 3
                TA, TB, TC = GT[tset], GT[tset + 1], GT[tset + 2]
                ka, kb, kc = ("GT", tset), ("GT", tset + 1), ("GT", tset + 2)
                pb = bank("fq")
                zf = psb[pb][:, 0:n]
                zq = psb[pb][:, 256:256 + n]
                for dk in range(8):
                    mm(zf, WINB[1][:, dk, h * 128:(h + 1) * 128], XTc[:, dk, 0:n], dk == 0, dk == 7,
                       ["WIN1", "XTc"], [PS(pb)])
                for dk in range(8):
                    mm(zq, WINB[0][:, dk, h * 128:(h + 1) * 128], XTc[:, dk, 0:n], dk == 0, dk == 7,
                       ["WIN0", "XTc"], [PS(pb)])
                lb_ap = LB[:, l, h:h + 1]
                oml_ap = OML[:, l, h:h + 1]
                lno_ap = LNOML[:, l, h:h + 1]
                a, b_, c = TA[:, 0:n], TB[:, 0:n], TC[:, 0:n]
                act(a, zf, AF.Exp, [PS(pb)], [ka], scale=-1.0)
                act(b_, a, AF.Ln, [ka], [kb], bias=1.0)
                act(a, b_, AF.Exp, [kb], [ka], scale=-1.0)
                act(a, a, AF.Ln, [ka, "LB", "OML"], [ka], scale=oml_ap, bias=lb_ap)
                S.add("dve", lambda e, c=c, a=a, RMv=RM[:, 0:n]: e.tensor_tensor_scan(
                    out=c, data0=RMv, data1=a, initial=0.0, op0=ALU.mult, op1=ALU.add),
                    [ka, "RMP", "RMS"], [kc], cost=100.0 + 2.1 * n)
                tt(b_, zf, b_, ALU.add, [PS(pb), kb], [kb])
                tt(b_, b_, c, ALU.add, [kb, kc], [kb])
                act(KT[:, h, 0:n], b_, AF.Exp, [kb, "LNOML"], ["KT"], scale=-1.0, bias=lno_ap)
                act(ELb[:, h, 0:G], c.rearrange("p (g c) -> p g c", c=blk)[:, :, blk - 1], AF.Exp, [kc], ["ELb"])
                act(a, zq, AF.Exp, [PS(pb)], [ka], scale=-1.0)
                act(a, a, AF.Ln, [ka], [ka], bias=1.0)
                tt(a, c, a, ALU.subtract, [kc, ka], [ka])
                act(a, a, AF.Exp, [ka], [ka])
                tt(QT[:, h, 0:n], zq, a, ALU.mult, [PS(pb), ka], ["QT"])
            for h in range(NH):
                tset = (h % 2) * 3
                TA = GT[tset]
                ka = ("GT", tset)
                a = TA[:, 0:n]
                pb = bank("g")
                zg = psb[pb][:, 0:n]
                for dk in range(8):
                    mm(zg, WINB[3][:, dk, h * 128:(h + 1) * 128], XTc[:, dk, 0:n], dk == 0, dk == 7,
                       ["WIN3", "XTc"], [PS(pb)])
                act(a, zg, AF.Exp, [PS(pb)], [ka], scale=-1.0)
                act(a, a, AF.Ln, [ka], [ka], bias=1.0)
                act(a, a, AF.Exp, [ka], [ka], scale=-1.0)
                stt(SG[:, h, 0:n], zg, ONG[:, l, h:h + 1], a, ALU.mult, ALU.mult, [PS(pb), ka, "ONG"], ["SG"])
            for j in range(ntl):
                pb = bank("v")
                zv = psb[pb][:, 0:512]
                for dk in range(8):
                    mm(zv, XTc[:, dk, j * 128:(j + 1) * 128], WINB[2][:, dk, :], dk == 0, dk == 7,
                       ["WIN2", "XTc"], [PS(pb)])
                act(VTOK[:, j, :], zv, AF.Copy, [PS(pb)], ["VTOK"])
                pb2 = bank("kt")
                ktp = psb[pb2][:, 0:256].bitcast(BF16)
                for h in range(NH):
                    tr(ktp[:, h * 128:(h + 1) * 128], KT[:, h, j * 128:(j + 1) * 128], identb, ["KT", "identb"],
                       [PS(pb2)])
                cp_(KTOK[:, j, :], ktp, [PS(pb2)], ["KTOK"], eng=KTOK_ENG)
            def rms_gate(ov, W, sgv, mixv, okeys, three_d):
                act(OSQ[:, 0:W], ov, AF.Square, okeys, ["OSQ"])
                pbQ = bank_excl("q")
                mm(psb[pbQ][:, 0:W], ones_s, OSQ[:, 0:W], True, True, ["ones_s", "OSQ"], [PS(pbQ)])
                rst = RST[:, 0:W]
                tmp = TMPO[:, 0:W]
                act(rst, psb[pbQ][:, 0:W], AF.Ln, [PS(pbQ)], [("GT", 0), ("GT", 1)], bias=RMS_EPS)
                act(rst, rst, AF.Exp, [("GT", 0), ("GT", 1)], [("GT", 0), ("GT", 1)], scale=-0.5)
                tt(tmp, ov, rst, ALU.mult, okeys + [("GT", 0), ("GT", 1)], [("GT", 2), ("GT", 3)])
                tv = tmp.rearrange("p (a b) -> p a b", a=NH) if three_d else tmp
                tt(mixv, sgv, tv, ALU.mult, ["SG", ("GT", 2), ("GT", 3)], ["MIX"])

            held = set()

            def bank_excl(cls=None):
                while True:
                    b = bank(cls if is_p else None)
                    if b not in held:
                        return b

            for j in range(ntl):
                c0, c1 = j * 128, (j + 1) * 128
                pbA = bank_excl("sc")
                for h in range(NH):
                    mm(psb[pbA][:, h * 128:(h + 1) * 128], KT[:, h, c0:c1], QT[:, h, c0:c1], True, True,
                       ["KT", "QT"], [PS(pbA)])
                at = ATm[j % 2]
                atk = ("ATm", j % 2)
                tt(at, psb[pbA][:, 0:512].rearrange("p (a b) -> p a b", a=NH),
                   mask.unsqueeze(1).to_broadcast([128, NH, 128]), ALU.mult, [PS(pbA), "maskP", "maskS"], [atk])
                if is_p:
                    grp_sb = []
                    for g in range(gpt):
                        gi = j * gpt + g
                        km = KM[g % 2]
                        kmk = ("KM", g % 2)
                        ts(km, KTOK[:, j, :], GM[:, g:g + 1], ALU.mult, ["KTOK", "GMP", "GMS"], [kmk], eng=KM_ENG)
                        pbP = bank_excl("P")
                        for h in range(NH):
                            mm(psb[pbP][:, h * 128:(h + 1) * 128], km[:, h * 128:(h + 1) * 128],
                               VTOK[:, j, h * 128:(h + 1) * 128], True, True, [kmk, "VTOK"], [PS(pbP)])
                        Pv = psb[pbP][:, 0:512].rearrange("p (a b) -> p a b", a=NH)
                        elv = ELb[:, :, gi:gi + 1].to_broadcast([128, NH, 128])
                        cur = sb_ctr[0] % 3
                        nxt = (sb_ctr[0] + 1) % 3
                        sb_ctr[0] += 1
                        grp_sb.append((Sb[cur], ("Sb", cur)))
                        tt(STMP, Pv, Sst, ALU.add, [PS(pbP), "Sst"], ["STMP"])
                        tt(Sst, STMP, elv, ALU.mult, ["STMP", "ELb"], ["Sst"])
                        cp_(Sb[nxt], Sst, ["Sst"], [("Sb", nxt)], eng="act")
                    pbO = bank_excl("o")
                    for h in range(NH):
                        oh = psb[pbO][:, h * 128:(h + 1) * 128]
                        mm(oh, VTOK[:, j, h * 128:(h + 1) * 128], at[:, h, :], True, False, ["VTOK", atk], [PS(pbO)])
                        for g in range(gpt):
                            sbv, sbk = grp_sb[g]
                            mm(oh[:, g * blk:(g + 1) * blk], sbv[:, h, :], QT[:, h, c0 + g * blk:c0 + (g + 1) * blk],
                               False, g == gpt - 1, [sbk, "QT"], [PS(pbO)])
                    rms_gate(psb[pbO][:, 0:512], 512, SG[:, :, c0:c1], MIX[:, 0:NH, c0:c1], [PS(pbO)], True)
                else:
                    pbO = bank_excl("o")
                    mm(psb[pbO][:, 0:512], zeros_b, KTOK[:, j, :], True, False, ["zeros_b", "KTOK"], [PS(pbO)])
                    for h in range(NH):
                        mm(psb[pbO][:, h * 128:(h + 1) * 128], VTOK[:, j, h * 128:(h + 1) * 128], at[:, h, :], False, False,
                           ["VTOK", atk], [PS(pbO)])

                    NSB = 4

                    def load_s0(g):
                        dma("sp", S0f[g % NSB], sh[l, g].rearrange("h k v -> k h v"), [],
                            [("S0f", g % NSB)] + (["S0GATE"] if g == 13 else []))

                    for g0 in range(NSB - 1):
                        load_s0(g0)
                    for g in range(gpt):
                        gi = j * gpt + g
                        s2 = g % 2
                        s4 = g % NSB
                        if g + NSB - 1 < gpt:
                            load_s0(g + NSB - 1)
                        cp_(S0b[s2], S0f[s4], [("S0f", s4)], [("S0b", s2)], eng="act")
                        for h in range(NH):
                            mm(psb[pbO][:, h * 128 + g * blk:h * 128 + (g + 1) * blk], S0b[s2][:, h, :],
                               QT[:, h, c0 + g * blk:c0 + (g + 1) * blk], False, g == gpt - 1 and h == NH - 1,
                               [("S0b", s2), "QT"], [PS(pbO)])
                        km = KM[g % 2]
                        kmk = ("KM", g % 2)
                        ts(km, KTOK[:, j, :], GM[:, g:g + 1], ALU.mult, ["KTOK", "GMP", "GMS"], [kmk], eng=KM_ENG)
                        pbP = bank_excl()
                        for h in range(NH):
                            mm(psb[pbP][:, h * 128:(h + 1) * 128], km[:, h * 128:(h + 1) * 128],
                               VTOK[:, j, h * 128:(h + 1) * 128], True, True, [kmk, "VTOK"], [PS(pbP)])
                        Pv = psb[pbP][:, 0:512].rearrange("p (a b) -> p a b", a=NH)
                        elv = ELb[:, :, gi:gi + 1].to_broadcast([128, NH, 128])
                        tt(STMP, Pv, S0f[s4], ALU.add, [PS(pbP), ("S0f", s4)], ["STMP"])
                        tt(S0f[s4], STMP, elv, ALU.mult, ["STMP", "ELb"], [("S0f", s4)])
                        dma("pool", hs[l, g].rearrange("h k v -> k h v"), S0f[s4], [("S0f", s4)], [], is_output=True)
                    rms_gate(psb[pbO][:, 0:512], 512, SG[:, :, c0:c1], MIX[:, 0:NH, c0:c1], [PS(pbO)], True)
            if is_p and tile0 + ntl == NPT:
                dma("sp", hp[l].rearrange("h k v -> k h v"), Sst, ["Sst"], [], is_output=True)

            ZH = GT[4][:, 0:n]
            ACC = GT[5][:, 0:n]
            kzh, kacc = ("GT", 4), ("GT", 5)
            for cc in range(4):
                pb = bank("cv")
                pb2 = bank("cv")
                zC = psb[pb][:, 0:n]
                zh = psb[pb][:, 256:256 + n]
                zB = psb[pb2][:, 0:n]
                for (zz, blk, pbx) in ((zC, 5, pb), (zh, 6, pb), (zB, 4, pb2)):
                    for dk in range(8):
                        mm(zz, WINB[blk][:, dk, cc * 128:(cc + 1) * 128], XTc[:, dk, 0:n], dk == 0, dk == 7,
                           ["WIN%d" % blk, "XTc"], [PS(pbx)])
                act(ZH, zh, AF.Copy, [PS(pb)], [kzh])
                CWK = ["CWT"]
                w0 = CWT[:, l, 0, cc:cc + 1]
                w1 = CWT[:, l, 1, cc:cc + 1]
                w2 = CWT[:, l, 2, cc:cc + 1]
                if is_p:
                    cp_(U[:, 0:2], CAR[:, cc, :], ["CAR"], ["U"], eng=CAR_ENG)
                    tt(U[:, 2:2 + n], zC, ZH, ALU.mult, [PS(pb), kzh, "U"], ["U"])
                    ts(ACC, U[:, 2:2 + n], w2, ALU.mult, ["U"] + CWK, [kacc])
                    stt(ACC, U[:, 1:1 + n], w1, ACC, ALU.mult, ALU.add, ["U", kacc] + CWK, [kacc])
                    stt(ACC, U[:, 0:n], w0, ACC, ALU.mult, ALU.add, ["U", kacc] + CWK, [kacc])
                    cp_(CAR[:, cc, :], U[:, n:n + 2], ["U"], ["CAR"], eng=CAR_ENG)
                else:
                    U3 = U[:, 0:160].rearrange("p (b t) -> p b t", t=10)
                    v3 = lambda ap: ap.rearrange("p (b t) -> p b t", t=8)
                    cp_(U3[:, :, 0:2], SCT[:, (l * 4 + cc) * 16:(l * 4 + cc + 1) * 16, :], [("SCT", l * 4 + cc)], ["U"])
                    tt(U3[:, :, 2:10], v3(zC), v3(ZH), ALU.mult, [PS(pb), kzh, "U"], ["U"])
                    ts(v3(ACC), U3[:, :, 2:10], w2, ALU.mult, ["U"] + CWK, [kacc])
                    stt(v3(ACC), U3[:, :, 1:9], w1, v3(ACC), ALU.mult, ALU.add, ["U", kacc] + CWK, [kacc])
                    stt(v3(ACC), U3[:, :, 0:8], w0, v3(ACC), ALU.mult, ALU.add, ["U", kacc] + CWK, [kacc])
                    cp_(CTAIL[:, cc * 16:(cc + 1) * 16, :], U3[:, :, 8:10], ["U"], ["CTAIL"])
                tt(MIX[:, NH + cc, 0:n], zB, ACC, ALU.mult, [PS(pb2), kacc], ["MIX"])
            if is_p and tile0 + ntl == NPT:
                pb = bank("xt")
                for cc in range(4):
                    tr(psb[pb][0:2, cc * 128:(cc + 1) * 128], CAR[:, cc, :], ident, ["CAR", "ident"], [PS(pb)])
                stg = GTALL[0:2, 0:512]
                cp_(stg, psb[pb][0:2, 0:512], [PS(pb)], [("GT", 0), ("GT", 1)])
                dma("sp", cp[l], stg, [("GT", 0), ("GT", 1)], [], is_output=True)
            if not is_p:
                pb = bank("xt")
                for cc in range(4):
                    tr(psb[pb][0:32, cc * 128:(cc + 1) * 128],
                       CTAIL[:, cc * 16:(cc + 1) * 16, :].rearrange("p a b -> p (a b)"), ident, ["CTAIL", "ident"],
                       [PS(pb)])
                stg = GTALL[0:32, 1024:1536]
                cp_(stg, psb[pb][0:32, 0:512], [PS(pb)], [("GT", 4), ("GT", 5)])
                dma("sp", cs[l].rearrange("b j c -> (b j) c"), stg, [("GT", 4), ("GT", 5)], [], is_output=True)

            if "mix" in taps and l == 0 and not is_p:
                dma("pool", dbg_mix, MIX[:, :, 0:128], ["MIX"], [], is_output=True)
                dma("pool", dbg_sg, SG[:, :, 0:128], ["SG"], [], is_output=True)
                dma("pool", dbg_qt, QT[:, :, 0:128], ["QT"], [], is_output=True)
                dma("pool", dbg_kt, KT[:, :, 0:128], ["KT"], [], is_output=True)
            for j in range(ntl):
                i = tile0 + j
                for half in range(2):
                    pb = bank("wo")
                    for mk in range(8):
                        mm(psb[pb][:, 0:512], MIX[:, mk, j * 128:(j + 1) * 128], WOUT[:, mk, half * 512:(half + 1) * 512],
                           mk == 0, mk == 7, ["MIX", "WOUT"], [PS(pb)])
                    Rh = R[:, i, half * 512:(half + 1) * 512]
                    stt(Rh, Rh, ALPHA, psb[pb][:, 0:512], ALU.mult, ALU.add, [("R", i), PS(pb)], [("R", i)])
                layer_norm_tile(i, l, False, affine=False)

        S.parity = None
        if ("m%d" % l) in taps:
            for i in range(NT):
                dma("sp", tap_out["m%d" % l][i], R[:, i, :], [("R", i)], [], is_output=True)

        dma("sp", LNG, ln1_g[l:l + 1, :].to_broadcast([128, D]), [], ["LNG"])
        dma("sp", LNB, ln1_b[l:l + 1, :].to_broadcast([128, D]), [], ["LNB"])
        ts(LNG, LNG, ALPHA, ALU.mult, ["LNG"], ["LNG"])
        ts(LNB, LNB, ALPHA, ALU.mult, ["LNB"], ["LNB"])

        def load_ffn(c):
            buf = c % 3
            gate = ["S0GATE"] if c < 3 else []
            dma("pool", F1[buf], w_ff1[l][:, c * 512:(c + 1) * 512].rearrange("(k p) n -> p k n", p=128), gate,
                ["F1_%d" % buf], max_dma_last_dim=8192)
            dma("pool", F2[buf], w_ff2[l][c * 512:(c + 1) * 512, :].rearrange("(k p) n -> p k n", p=128), gate,
                ["F2_%d" % buf], max_dma_last_dim=8192)

        def xt_rhs(tile0_, ntl_, dk_):
            if ntl_ == 1:
                return XTt[tile0_][:, dk_, :]
            base = XTt[tile0_]
            flat = arena[:, 0:ARENA_BYTES // 4].bitcast(BF16)
            off = xt_off[tile0_] + dk_ * 128
            return flat[:, off:off + ntl_ * 1024].rearrange("p (t r) -> p t r", r=1024)[:, :, 0:128]

        fsup = []
        t = 0
        while t < NPT:
            k = min(4, NPT - t)
            fsup.append((t, k))
            t += k
        fsup.append((NPT, 1))
        hctr = [0]
        load_ffn(0)
        load_ffn(1)
        for c in range(8):
            if c + 2 < 8:
                load_ffn(c + 2)
            if c == 1:
                dma("sp", LNG, ln2_g[l:l + 1, :].to_broadcast([128, D]), [], ["LNG"])
                dma("sp", LNB, ln2_b[l:l + 1, :].to_broadcast([128, D]), [], ["LNB"])
            buf = c % 3
            sup_c = fsup if c < 7 else [(NPT, 1)] + [(t_, 2) for t_ in range(0, NPT, 2)]
            for (tile0, ntl) in sup_c:
                n = 128 * ntl
                t0 = tile0 * 128
                if c == 0:
                    for j in range(ntl):
                        gen_xt(tile0 + j, XTt[tile0 + j], "XT%d" % (tile0 + j), affine_l=l)
                xtk = ["XT%d" % (tile0 + j) for j in range(ntl)]
                hb = HID[hctr[0] % 2]
                hbk = ("HID", hctr[0] % 2)
                hctr[0] += 1
                for fc in range(4):
                    pb = bank("f1")
                    for dk in range(8):
                        mm(psb[pb][:, 0:n], F1[buf][:, dk, fc * 128:(fc + 1) * 128], xt_rhs(tile0, ntl, dk), dk == 0,
                           dk == 7, ["F1_%d" % buf] + xtk, [PS(pb)])
                    sq = SQ[fc % 2]
                    sqk = ("SQ", fc % 2)
                    act(sq[:, 0:n], psb[pb][:, 0:n], AF.Square, [PS(pb)], [sqk])
                    stt(hb[:, fc, 0:n], psb[pb][:, 0:n], 0.0, sq[:, 0:n], ALU.is_gt, ALU.mult, [PS(pb), sqk], [hbk])
                for j in range(ntl):
                    i = tile0 + j
                    if c == 0:
                        tt(R[:, i, :], R[:, i, :], LNG, ALU.mult, [("R", i), "LNG"], [("R", i)], f=1.0)
                        tt(R[:, i, :], R[:, i, :], LNB, ALU.add, [("R", i), "LNB"], [("R", i)], f=1.0)
                    for half in range(2):
                        pb = bank("f2")
                        for fk in range(4):
                            mm(psb[pb][:, 0:512], hb[:, fk, j * 128:(j + 1) * 128],
                               F2[buf][:, fk, half * 512:(half + 1) * 512], fk == 0, fk == 3, [hbk, "F2_%d" % buf],
                               [PS(pb)])
                        Rh = R[:, i, half * 512:(half + 1) * 512]
                        tt(Rh, psb[pb][:, 0:512], Rh, ALU.add, [("R", i), PS(pb)], [("R", i)])
                    if c == 7:
                        layer_norm_tile(i, l, l == DEPTH - 1, affine_eng=LN2_ENG[i % len(LN2_ENG)])
        if ("f%d" % l) in taps:
            for i in range(NT):
                dma("sp", tap_out["f%d" % l][i], R[:, i, :], [("R", i)], [], is_output=True)

    if SCHEDULE:
        S.schedule()
        print("[kernel] simulated time (us):", S.sim_time / 1e3)
    stats = S.emit(nc, sems, dma_sems)
    print("[kernel] ops (instr, waits):", stats)
    es.close()
    return nc


_PROG_CACHE = {}


def _get_prog(n_ptiles=16, taps=()):
    key = (n_ptiles, tuple(taps))
    if key not in _PROG_CACHE:
        _PROG_CACHE[key] = build_program(n_ptiles, taps)
    return _PROG_CACHE[key]


def _in_maps(inp, n_ptiles=16):
    seql = n_ptiles * 128
    f = lambda a: np.ascontiguousarray(np.asarray(a, dtype=np.float32))
    shared = {k: f(inp[k]) for k in ("w_in", "lb_logits", "conv_w", "onorm_g", "w_out", "ln1_g", "ln1_b", "w_ff1",
                                     "w_ff2", "ln2_g", "ln2_b")}
    maps = []
    for c in range(N_CORES):
        m = dict(shared)
        m["xp"] = f(inp["x_prompt"][c, :seql])
        m["xs"] = f(np.asarray(inp["x_sample"])[16 * c:16 * (c + 1)].reshape(128, D))
        m["sh"] = f(np.asarray(inp["state_hgrn"])[:, 16 * c:16 * (c + 1)])
        m["sc"] = f(np.asarray(inp["state_conv"])[:, 16 * c:16 * (c + 1)])
        maps.append(m)
    return maps


def kernel(**inputs):
    nc = _get_prog()
    res = run_bass_kernel_spmd(nc, _in_maps(inputs), core_ids=list(range(N_CORES)))
    r = res.results
    y_prompt = np.stack([r[c]["yp"] for c in range(N_CORES)], axis=0).astype(np.float32)
    y_sample = np.concatenate([r[c]["ys"].reshape(16, DEC_T, D) for c in range(N_CORES)], axis=0).astype(np.float32)
    new_hgrn_prompt = np.stack([r[c]["hp"] for c in range(N_CORES)], axis=1).astype(np.float32)
    new_conv_prompt = np.stack([r[c]["cp"] for c in range(N_CORES)], axis=1).astype(np.float32)
    new_hgrn_sample = np.concatenate([r[c]["hs"] for c in range(N_CORES)], axis=1).astype(np.float32)
    new_conv_sample = np.concatenate([r[c]["cs"] for c in range(N_CORES)], axis=1).astype(np.float32)
    return (y_prompt, y_sample, new_hgrn_prompt, new_conv_prompt, new_hgrn_sample, new_conv_sample)
```

```python
from contextlib import ExitStack

import numpy as np
import concourse.bass as bass
import concourse.mybir as mybir
from concourse.bass_utils import run_bass_kernel_spmd

F32 = mybir.dt.float32
BF16 = mybir.dt.bfloat16
AF = mybir.ActivationFunctionType
ALU = mybir.AluOpType

N_CORES = 8
D = 1024
DEPTH = 2
SEQ = 2048
DEC_B = 128
DEC_T = 8
HW = 512
CW_ = 512
NH = 4
PW = 3584
DFF = 4096
ALPHA = float((2 * DEPTH) ** 0.25)
LN_EPS = 1e-5
RMS_EPS = 1e-6
N_DMA_SEMS = 16
SCHEDULE = True
import os as _os
AHEAD_RESERVE = 0
SYNC_LAT = float(_os.environ.get('SYNC_LAT', '250'))
import os as _os
LN2_ENG = tuple(_os.environ.get('LN2_ENG', 'dve,pool').split(','))
KTOK_ENG = _os.environ.get('KTOK_ENG', 'act')
KM_ENG = _os.environ.get('KM_ENG', 'dve')
CAR_ENG = _os.environ.get('CAR_ENG', 'dve')
LN1_ENG = tuple(_os.environ.get('LN1_ENG', 'pool').split(','))
SIM_NOFENCE = bool(_os.environ.get('SIM_NOFENCE'))
SIM_DB = _os.environ.get('SIM_DB', '')


class LazyAP:
    phys = None
    bankmap = None

    def __init__(self, tid, ops=()):
        self.tid = tid
        self.ops = ops

    def __getitem__(self, key):
        return LazyAP(self.tid, self.ops + (("g", key),))

    def rearrange(self, s, **kw):
        return LazyAP(self.tid, self.ops + (("r", s, kw),))

    def bitcast(self, dt):
        return LazyAP(self.tid, self.ops + (("b", dt),))

    def _apply(self, t):
        for o in self.ops:
            if o[0] == "g":
                t = t[o[1]]
            elif o[0] == "r":
                t = t.rearrange(o[1], **o[2])
            else:
                t = t.bitcast(o[1])
        return t

    @property
    def shape(self):
        return self._apply(LazyAP.phys[0]).shape

    def resolve(self):
        return self._apply(LazyAP.phys[LazyAP.bankmap[self.tid]])


class _LazyBanks:
    def __getitem__(self, tid):
        return LazyAP(tid)


def rs(x):
    return x.resolve() if isinstance(x, LazyAP) else x


class _Op:
    __slots__ = ("eng", "fn", "dma", "idx", "deps", "sig", "semval", "dsem", "dval", "prev_dval", "deps_all",
                 "cost", "nbytes", "desc", "cost_start", "cost_fin")

    def __init__(self, eng, fn, dma, idx):
        self.eng = eng
        self.fn = fn
        self.dma = dma
        self.idx = idx
        self.deps = ()
        self.deps_all = ()
        self.cost = 100.0
        self.nbytes = 0
        self.sig = False
        self.semval = 0
        self.dsem = None
        self.dval = 0
        self.prev_dval = 0


class Sched:
    ENGS = ("pe", "act", "dve", "pool", "sp")

    def __init__(self):
        self.ops = []
        self.last_w = {}
        self.readers = {}
        self.out_dmas = []
        self.ranges = {}
        self.alias = None
        self.subkeys = {}
        self.epoch = {}
        self.absorbed = {}

    def reg(self, name, lo, hi):
        if name in self.ranges:
            a, b = self.ranges[name]
            lo, hi = min(a, lo), max(b, hi)
        self.ranges[name] = (lo, hi)

    def _aliases(self, key):
        name = key if isinstance(key, str) else key[0]
        if self.alias is None:
            self.alias = {}
            items = list(self.ranges.items())
            for n, (lo, hi) in items:
                self.alias[n] = [m for m, (a, b) in items if m != n and a < hi and lo < b]
        self.subkeys.setdefault(name, set()).add(key)
        out = []
        for m in self.alias.get(name, ()):
            out.extend(self.subkeys.get(m, ()))
        return out

    def add(self, eng, fn, reads=(), writes=(), dma=False, is_output=False, cost=100.0, nbytes=0):
        if SIM_DB and getattr(self, "parity", None) is not None:
            ren = lambda k: (k, self.parity) if (k if isinstance(k, str) else k[0]) in SIM_DB.split(",") else k
            reads = [ren(k) for k in reads]
            writes = [ren(k) for k in writes]
        idx = len(self.ops)
        psr = [k for k in reads if isinstance(k, tuple) and k[0] == "ps"]
        if psr:
            writes = list(writes) + [k for k in psr if k not in writes]
        op = _Op(eng, fn, dma, idx)
        op.cost = cost
        op.nbytes = nbytes
        op.desc = (tuple(reads), tuple(writes))
        deps = set()
        for k in reads:
            self._aliases(k)
            w = self.last_w.get(k)
            if w is not None:
                deps.add(w)
        for k in writes:
            w = self.last_w.get(k)
            if w is not None:
                deps.add(w)
            for r in self.readers.get(k, ()):
                deps.add(r)
            for k2 in self._aliases(k):
                ep = self.epoch.get(k2, 0)
                if self.absorbed.get((k, k2)) == ep:
                    continue
                self.absorbed[(k, k2)] = ep
                w = self.last_w.get(k2)
                if w is not None:
                    deps.add(w)
                for r in self.readers.get(k2, ()):
                    deps.add(r)
        deps.discard(idx)
        for k in reads:
            self.readers.setdefault(k, []).append(idx)
            self.epoch[k] = self.epoch.get(k, 0) + 1
        for k in writes:
            self.last_w[k] = idx
            self.readers[k] = []
            self.epoch[k] = self.epoch.get(k, 0) + 1
        op.deps_all = tuple(sorted(deps))
        op.deps = tuple(d for d in op.deps_all
                        if not (eng == "pe" and not dma and self.ops[d].eng == "pe" and not self.ops[d].dma))
        self.ops.append(op)
        if is_output:
            self.out_dmas.append(idx)
        return idx

    def schedule(self, window=48, sync_lat=SYNC_LAT, dma_lat=2200.0, dma_bw=180.0):
        ops = self.ops
        n = len(ops)
        succ = [[] for _ in range(n)]
        cnt = [0] * n
        for op in ops:
            cnt[op.idx] = len(op.deps_all)
            for d in op.deps_all:
                succ[d].append(op.idx)
        rtime = [0.0] * n
        crit_dep = {}
        op_prev_on_eng = {}
        last_on_eng = {}
        self.crit_dep = crit_dep
        self.op_prev_on_eng = op_prev_on_eng
        tile_ops = {}
        for op in ops:
            for k in set(op.desc[0]) | set(op.desc[1]):
                if isinstance(k, tuple) and k[0] == "ps":
                    tile_ops.setdefault(k[1], []).append(op.idx)
        alloc_order = sorted(tile_ops, key=lambda t: tile_ops[t][0])
        first_of = {tile_ops[t][0]: t for t in alloc_order}
        op_tiles = {}
        for t, lst in tile_ops.items():
            for i in lst:
                op_tiles.setdefault(i, []).append(t)
        next_alloc = [0]
        NB = 8
        ev = []
        for t, lst in tile_ops.items():
            ev.append((lst[0], 1))
            ev.append((lst[-1] + 0.5, -1))
        live = mx = 0
        for _, d in sorted(ev):
            live += d
            mx = max(mx, live)
        ahead_max = max(0, NB - mx - AHEAD_RESERVE)
        self.max_live = mx
        rank = {t: i for i, t in enumerate(alloc_order)}
        allocated = set()
        tenant = [None] * NB
        bank_free = [0.0] * NB
        remaining = {t: len(lst) for t, lst in tile_ops.items()}
        tile_fin = {t: 0.0 for t in tile_ops}
        bankmap = {}
        pend = {e: [] for e in self.ENGS}
        for op in ops:
            pend[op.eng].append(op.idx)
        head = {e: 0 for e in self.ENGS}
        done = {e: [False] * len(pend[e]) for e in self.ENGS}
        etime = {e: 0.0 for e in self.ENGS}
        dma_free = [0.0]
        order = {e: [] for e in self.ENGS}
        left = n
        tmax = 0.0
        while left:
            best = None
            for e in self.ENGS:
                lst = pend[e]
                dn = done[e]
                h = head[e]
                while h < len(lst) and dn[h]:
                    h += 1
                head[e] = h
                seen = 0
                k = h
                et = etime[e]
                while k < len(lst) and seen < window:
                    if not dn[k]:
                        seen += 1
                        i = lst[k]
                        if cnt[i] == 0:
                            rdy = rtime[i] if rtime[i] > et else et
                            t_new = first_of.get(i)
                            if t_new is not None:
                                if alloc_order[next_alloc[0]] != t_new:
                                    n_ahead = sum(1 for b in range(NB) if tenant[b] is not None
                                                  and remaining[tenant[b]] > 0 and rank[tenant[b]] > next_alloc[0])
                                    if n_ahead >= ahead_max:
                                        k += 1
                                        continue
                                fb = None
                                for b in range(NB):
                                    if tenant[b] is None or remaining[tenant[b]] == 0:
                                        if fb is None or bank_free[b] < bank_free[fb]:
                                            fb = b
                                if fb is None:
                                    k += 1
                                    continue
                                if bank_free[fb] + sync_lat > rdy:
                                    rdy = bank_free[fb] + sync_lat
                            cand = (rdy + 0.02 * (k - h), i, e, k, rdy)
                            if best is None or cand < best:
                                best = cand
                    k += 1
            assert best is not None, "scheduler deadlock"
            _, idx, e, k, rdy = best
            op = ops[idx]
            t_new = first_of.get(idx)
            if t_new is not None:
                fb = None
                for b in range(NB):
                    if tenant[b] is None or remaining[tenant[b]] == 0:
                        if fb is None or bank_free[b] < bank_free[fb]:
                            fb = b
                prev = tenant[fb]
                if prev is not None:
                    extra = tuple(tile_ops[prev])
                    op.deps_all = tuple(sorted(set(op.deps_all) | set(extra)))
                    op.deps = tuple(d for d in op.deps_all
                                    if not (op.eng == "pe" and not op.dma and ops[d].eng == "pe" and not ops[d].dma))
                tenant[fb] = t_new
                bankmap[t_new] = fb
                allocated.add(t_new)
                while next_alloc[0] < len(alloc_order) and alloc_order[next_alloc[0]] in allocated:
                    next_alloc[0] += 1
            if op.dma:
                if op.nbytes <= 600000:
                    f = rdy + dma_lat + 2.0 * op.nbytes / dma_bw
                else:
                    st = max(rdy, dma_free[0])
                    dma_free[0] = st + op.nbytes / dma_bw
                    f = dma_free[0] + dma_lat
                etime[e] = rdy + 60.0
            else:
                f = rdy + op.cost
                etime[e] = f
            op.cost_start = rdy
            op.cost_fin = f
            if f > tmax:
                tmax = f
            for t in op_tiles.get(idx, ()):
                remaining[t] -= 1
                if f > tile_fin[t]:
                    tile_fin[t] = f
                if remaining[t] == 0:
                    bank_free[bankmap[t]] = tile_fin[t]
            for s in succ[idx]:
                cnt[s] -= 1
                if f + sync_lat > rtime[s]:
                    rtime[s] = f + sync_lat
                    crit_dep[s] = idx
            op_prev_on_eng[idx] = last_on_eng.get(e)
            last_on_eng[e] = idx
            done[e][k] = True
            order[e].append(idx)
            left -= 1
        self.order = order
        self.sim_time = tmax
        LazyAP.bankmap = bankmap
        return order

    def emit(self, nc, sems, dma_sems):
        ops = self.ops
        order = getattr(self, "order", None)
        if order is None:
            order = {e: [op.idx for op in ops if op.eng == e] for e in self.ENGS}
        for op in ops:
            for d in op.deps:
                ops[d].sig = True
        cnt = {e: 0 for e in self.ENGS}
        dcnt = {e: 0 for e in self.ENGS}
        for op in (ops[i] for e in self.ENGS for i in order[e]):
            if op.dma:
                j = dcnt[op.eng]
                dcnt[op.eng] += 1
                pool = dma_sems[op.eng]
                op.dsem = pool[j % len(pool)]
                op.dval = 16 * (j // len(pool) + 1)
                op.prev_dval = 16 * (j // len(pool))
            elif op.sig:
                cnt[op.eng] += 1
                op.semval = cnt[op.eng]
        stats = {e: [0, 0] for e in self.ENGS}

        def run(ename, e):
            seen = {}

            def wait(sem, val):
                key = id(sem)
                if seen.get(key, 0) >= val:
                    return
                e.wait_ge(sem, val)
                seen[key] = val
                stats[ename][1] += 1

            for op in (ops[i] for i in order[ename]):
                need = {}
                for d in op.deps:
                    dop = ops[d]
                    sm, vv = (dop.dsem, dop.dval) if dop.dma else (sems[dop.eng], dop.semval)
                    if need.get(id(sm), (None, 0))[1] < vv:
                        need[id(sm)] = (sm, vv)
                for sm, vv in need.values():
                    wait(sm, vv)
                if op.dma:
                    if op.prev_dval > 0:
                        wait(op.dsem, op.prev_dval)
                    ins = op.fn(e)
                    ins.then_inc(op.dsem, 16)
                else:
                    ins = op.fn(e)
                    if op.sig:
                        ins.then_inc(sems[op.eng], 1)
                stats[ename][0] += 1
            if ename == "sp":
                for d in self.out_dmas:
                    dop = ops[d]
                    wait(dop.dsem, dop.dval)

        with nc.Block() as block:
            @block.tensor
            def _(e):
                run("pe", e)

            @block.scalar
            def _(e):
                run("act", e)

            @block.vector
            def _(e):
                run("dve", e)

            @block.gpsimd
            def _(e):
                run("pool", e)

            @block.sync
            def _(e):
                run("sp", e)
        return stats


def build_program(n_ptiles=16, taps=()):
    assert n_ptiles % 2 == 0
    NPT = n_ptiles
    NT = NPT + 1
    SEQL = NPT * 128
    nc = bass.Bass("TRN2", target_bir_lowering=False)

    def din(name, shape):
        return nc.dram_tensor(name, list(shape), F32, kind="ExternalInput").ap()

    def dout(name, shape):
        return nc.dram_tensor(name, list(shape), F32, kind="ExternalOutput").ap()

    xp = din("xp", [SEQL, D])
    xs = din("xs", [128, D])
    sh = din("sh", [DEPTH, 16, NH, 128, 128])
    sc = din("sc", [DEPTH, 16, 2, CW_])
    w_in = din("w_in", [DEPTH, D, PW])
    lb_logits = din("lb_logits", [DEPTH, HW])
    conv_w = din("conv_w", [DEPTH, 3, CW_])
    onorm_g = din("onorm_g", [DEPTH, HW])
    w_out = din("w_out", [DEPTH, D, D])
    ln1_g = din("ln1_g", [DEPTH, D])
    ln1_b = din("ln1_b", [DEPTH, D])
    w_ff1 = din("w_ff1", [DEPTH, D, DFF])
    w_ff2 = din("w_ff2", [DEPTH, DFF, D])
    ln2_g = din("ln2_g", [DEPTH, D])
    ln2_b = din("ln2_b", [DEPTH, D])

    yp = dout("yp", [SEQL, D])
    ys = dout("ys", [128, D])
    hp = dout("hp", [DEPTH, NH, 128, 128])
    cp = dout("cp", [DEPTH, 2, CW_])
    hs = dout("hs", [DEPTH, 16, NH, 128, 128])
    cs = dout("cs", [DEPTH, 16, 2, CW_])
    tap_out = {t: dout("tap_" + t, [NT, 128, D]) for t in taps if t != "mix"}
    if "mix" in taps:
        dbg_mix = dout("dbg_mix", [128, 8, 128])
        dbg_sg = dout("dbg_sg", [128, 4, 128])
        dbg_qt = dout("dbg_qt", [128, 4, 128])
        dbg_kt = dout("dbg_kt", [128, 4, 128])

    es = ExitStack()
    ARENA_BYTES = 211968
    arena = es.enter_context(nc.sbuf_tensor("arena", [128, ARENA_BYTES // 4], F32))
    LazyAP.phys = [es.enter_context(nc.psum_tensor(f"ps{i}", [128, 512], F32)) for i in range(8)]
    psb = _LazyBanks()
    sems = {e: es.enter_context(nc.semaphore(f"s_{e}")) for e in Sched.ENGS}
    dma_sems = {e: [es.enter_context(nc.semaphore(f"d_{e}{i}")) for i in range(N_DMA_SEMS if e == "sp" else 6)]
                for e in ("sp", "pool")}
    for e in ("act", "pe", "dve"):
        dma_sems[e] = dma_sems["sp"]

    class Arena:
        def __init__(self, base, limit):
            self.off = base
            self.limit = limit

        def alloc(self, shape, dt, name=None):
            nel = int(np.prod(shape[1:]))
            nbytes = nel * (4 if dt == F32 else 2)
            nbytes = (nbytes + 31) // 32 * 32
            o = self.off
            self.off += nbytes
            assert self.off <= self.limit, (self.off, self.limit, shape)
            if name is not None:
                S.reg(name, o, o + nbytes)
            v = arena[:, o // 4:(o + nbytes) // 4]
            if dt != F32:
                v = v.bitcast(dt)
            v = v[:, 0:nel]
            if len(shape) == 3:
                v = v.rearrange("p (a b) -> p a b", a=shape[1])
            elif len(shape) == 4:
                v = v.rearrange("p (a b c) -> p a b c", a=shape[1], b=shape[2])
            return v

    S = Sched()
    A0 = Arena(0, ARENA_BYTES)
    R = A0.alloc([128, NT, D], F32)
    ident = A0.alloc([128, 128], F32)
    identb = A0.alloc([128, 128], BF16)
    ones_s = A0.alloc([128, 128], BF16)
    zeros_b = A0.alloc([128, 128], BF16)
    maskP = A0.alloc([128, 128], F32)
    maskS = A0.alloc([128, 128], F32)
    RMP = A0.alloc([128, 256], F32)
    RMS = A0.alloc([128, 128], F32)
    GMP = A0.alloc([128, 2], F32)
    GMS = A0.alloc([128, 16], F32)
    LBR = A0.alloc([128, DEPTH, NH], F32)
    LB = A0.alloc([128, DEPTH, NH], F32)
    OML = A0.alloc([128, DEPTH, NH], F32)
    LNOML = A0.alloc([128, DEPTH, NH], F32)
    LBT = A0.alloc([128, NH], F32)
    CWT = A0.alloc([128, DEPTH, 3, 4], F32)
    ONG = A0.alloc([128, DEPTH, NH], F32)
    SCT = A0.alloc([128, DEPTH * 4 * 16, 2], F32)
    CTAIL = A0.alloc([128, 4 * 16, 2], F32)
    CAR = A0.alloc([128, 4, 2], F32)
    FSCR = A0.alloc([128, 8], F32)
    G1c = A0.alloc([128, DEPTH, 8], F32)
    B1c = A0.alloc([128, DEPTH, 8], F32)
    LNG = A0.alloc([128, D], F32)
    LNB = A0.alloc([128, D], F32)
    LNS = [dict(ST=A0.alloc([128, 12], F32), MV=A0.alloc([128, 2], F32), LNV=A0.alloc([128, 1], F32),
                RSTD=A0.alloc([128, 1], F32), NMR=A0.alloc([128, 1], F32)) for _ in range(4)]
    persist_end = A0.off
    XT_BYTES = 8 * NT * 128 * 2
    P0 = persist_end
    NS = 256
    AM = Arena(P0, ARENA_BYTES)
    WINB = [AM.alloc([128, 8, 512], BF16, "WIN%d" % b) for b in range(7)]
    WOUT = AM.alloc([128, 8, D], BF16, "WOUT")
    XTc = AM.alloc([128, 8, NS], BF16, "XTc")
    QT = AM.alloc([128, NH, NS], BF16, "QT")
    KT = AM.alloc([128, NH, NS], BF16, "KT")
    KTOK = AM.alloc([128, 2, 512], BF16, "KTOK")
    VTOK = AM.alloc([128, 2, 512], BF16, "VTOK")
    GTALL = AM.alloc([128, 6 * NS], F32, "GT")
    GT = [GTALL[:, i * NS:(i + 1) * NS] for i in range(6)]
    RST = GTALL[:, 0:512]
    TMPO = GTALL[:, 512:1024]
    SG = AM.alloc([128, NH, NS], F32, "SG")
    ELb = AM.alloc([128, NH, 16], F32, "ELb")
    early_end = AM.off
    ATm = [AM.alloc([128, NH, 128], BF16, "ATm") for _ in range(2)]
    KM = [AM.alloc([128, 512], BF16, "KM") for _ in range(2)]
    Sst = AM.alloc([128, NH, 128], F32, "Sst")
    STMP = AM.alloc([128, NH, 128], F32, "STMP")
    Sb = [AM.alloc([128, NH, 128], BF16, "Sb") for _ in range(3)]
    S0f = [AM.alloc([128, NH, 128], F32, "S0f") for _ in range(4)]
    S0b = [AM.alloc([128, NH, 128], BF16, "S0b") for _ in range(2)]
    late0 = AM.off
    OSQ = AM.alloc([128, 512], BF16, "OSQ")
    MIX = AM.alloc([128, 8, NS], BF16, "MIX")
    U = AM.alloc([128, NS + 8], F32, "U")
    mixer_end = AM.off
    AF_ = Arena(P0, ARENA_BYTES)
    F1 = [None] * 3
    F2 = [None] * 3
    F1[0] = AF_.alloc([128, 8, 512], BF16, "F1_0")
    F2[0] = AF_.alloc([128, 4, D], BF16, "F2_0")
    XTt = [AF_.alloc([128, 8, 128], BF16, "XT%d" % i_) for i_ in range(NT)]
    xt_off = [(P0 + 2 * 8192 + i_ * 2048) // 2 for i_ in range(NT)]
    SQ = [AF_.alloc([128, 512], F32, "SQ") for _ in range(2)]
    assert AF_.off <= P0 + 7 * 8192, AF_.off
    AF_.off = P0 + 7 * 8192
    F1[1] = AF_.alloc([128, 8, 512], BF16, "F1_1")
    F2[1] = AF_.alloc([128, 4, D], BF16, "F2_1")
    F1[2] = AF_.alloc([128, 8, 512], BF16, "F1_2")
    F2[2] = AF_.alloc([128, 4, D], BF16, "F2_2")
    assert AF_.off <= early_end, (AF_.off, early_end)
    AF_.off = late0
    HID = [AF_.alloc([128, 4, 512], BF16, "HID") for _ in range(2)]
    ffn_end = AF_.off
    print(f"[kernel] arena: persist={persist_end} mixer_end={mixer_end} ffn_end={ffn_end} limit={ARENA_BYTES}")

    bank_ctr = [0]

    SIM_INFBANK = bool(_os.environ.get("SIM_INFBANK"))
    uniq = [0]

    cur_uid = {}

    def bank(cls=None):
        uniq[0] += 1
        return uniq[0]

    def fsz(ap):
        return int(np.prod(ap.shape[1:]))

    def mm(out, lhsT, rhs, start, stop, reads, writes):
        S.add("pe", lambda e: e.matmul(rs(out), lhsT=lhsT, rhs=rhs, start=start, stop=stop), reads, writes,
              cost=max(64, fsz(rhs)) / 2.4 + 8.0)

    def tr(out, in_, idn, reads, writes):
        S.add("pe", lambda e: e.transpose(rs(out), in_, idn), reads, writes, cost=110.0)

    def act(out, in_, func, reads, writes, scale=None, bias=None):
        kw = {}
        if scale is not None:
            kw["scale"] = scale
        if bias is not None:
            kw["bias"] = bias
        S.add("act", lambda e: e.activation(out=out, in_=rs(in_), func=func, **kw), reads, writes,
              cost=220.0 + 0.83 * fsz(in_))

    def ecost(eng, n, f):
        return (100.0 + 1.04 * f * n) * (2.0 if eng == "pool" else 1.0)

    def tt(out, a, b, op, reads, writes, eng="dve", f=1.5):
        S.add(eng, lambda e: e.tensor_tensor(out=out, in0=rs(a), in1=rs(b), op=op), reads, writes,
              cost=ecost(eng, fsz(out), f))

    def ts(out, in0, s1, op0, reads, writes, s2=None, op1=None, eng="dve"):
        c = ecost(eng, fsz(out), 1.0)
        if op1 is None:
            S.add(eng, lambda e: e.tensor_scalar(out=out, in0=rs(in0), scalar1=s1, scalar2=None, op0=op0), reads, writes,
                  cost=c)
        else:
            S.add(eng, lambda e: e.tensor_scalar(out=out, in0=rs(in0), scalar1=s1, scalar2=s2, op0=op0, op1=op1),
                  reads, writes, cost=c)

    def stt(out, in0, scalar, in1, op0, op1, reads, writes):
        S.add("dve", lambda e: e.scalar_tensor_tensor(out=out, in0=rs(in0), scalar=scalar, in1=rs(in1), op0=op0,
                                                        op1=op1),
              reads, writes, cost=ecost("dve", fsz(out), 1.3))

    def cp_(out, in_, reads, writes, eng="dve"):
        if eng == "act":
            S.add("act", lambda e: e.copy(out=out, in_=rs(in_)), reads, writes, cost=220.0 + 0.83 * fsz(out))
        else:
            S.add(eng, lambda e: e.tensor_copy(out=out, in_=rs(in_)), reads, writes, cost=ecost(eng, fsz(out), 1.0))

    def memset(ap, val, reads, writes, eng="pool"):
        S.add(eng, lambda e: e.memset(ap, val), reads, writes, cost=ecost(eng, fsz(ap), 1.0))

    def dma(eng, out, in_, reads, writes, is_output=False, **kw):
        nb = 128 * fsz(out) * 4
        S.add(eng, lambda e: e.dma_start(out=out, in_=in_, **kw), reads, writes, dma=True, is_output=is_output,
              nbytes=nb)

    def asel(ap, pattern, cm, base, reads, writes):
        S.add("pool", lambda e: e.affine_select(out=ap, in_=ap, pattern=pattern, compare_op=ALU.is_ge, fill=0.0,
                                                 base=base, channel_multiplier=cm), reads, writes, cost=400.0)

    PS = lambda b: ("ps", b)

    memset(ident, 1.0, [], ["ident"])
    S.add("pool", lambda e: e.affine_select(out=ident, in_=ident, pattern=[[-1, 128]], compare_op=ALU.is_equal,
                                             fill=0.0, base=0, channel_multiplier=1), ["ident"], ["ident"])
    cp_(identb, ident, ["ident"], ["identb"], eng="pool")
    memset(ones_s, 1.0 / 128.0, [], ["ones_s"])
    memset(zeros_b, 0.0, [], ["zeros_b"])
    for (mk, blk, nm) in ((maskP, 64, "maskP"), (maskS, 8, "maskS")):
        nb = 128 // blk
        memset(mk, 1.0, [], [nm])
        v3 = mk.rearrange("p (a b) -> p a b", b=blk)
        asel(v3, [[blk, nb], [1, blk]], -1, 0, [nm], [nm])
        asel(v3, [[-blk, nb], [0, blk]], 1, 0, [nm], [nm])
    memset(RMP, 1.0, [], ["RMP"])
    memset(RMP.rearrange("p (a b) -> p a b", b=64)[:, :, 0:1], 0.0, ["RMP"], ["RMP"])
    memset(RMS, 1.0, [], ["RMS"])
    memset(RMS.rearrange("p (a b) -> p a b", b=8)[:, :, 0:1], 0.0, ["RMS"], ["RMS"])
    for (gm, blk, nm) in ((GMP, 64, "GMP"), (GMS, 8, "GMS")):
        nb = 128 // blk
        memset(gm, 1.0, [], [nm])
        asel(gm, [[-blk, nb]], 1, 0, [nm], [nm])
        asel(gm, [[blk, nb]], -1, blk - 1, [nm], [nm])
    STG = LNG
    STGK = [("STGR", i_) for i_ in range(6)]
    memset(STG[0:96, :], 0.0, [], ["LNG"] + STGK, eng="dve")
    dma("sp", STG[0:2, 0:512], lb_logits, [], [STGK[0]])
    dma("sp", STG[2:4, 0:512], onorm_g, [], [STGK[1]])
    dma("sp", STG[4:10, 0:512], conv_w.rearrange("l j c -> (l j) c"), [], [STGK[2]])
    dma("sp", STG[10:12, :], ln1_g, [], [STGK[3]])
    dma("sp", STG[12:14, :], ln1_b, [], [STGK[4]])
    dma("sp", STG[16:80, 0:512], sc.rearrange("l b j c -> (l b j) c"), [], [STGK[5]])
    for k in range(8):
        pb = bank("xt")
        pv = psb[pb][:, 0:80]
        tr(pv, STG[0:80, k * 128:(k + 1) * 128], ident[0:80, 0:80], ["LNG", "ident"] + STGK, [PS(pb)])
        cp_(G1c[:, :, k], psb[pb][:, 10:12], [PS(pb)], ["G1c"])
        cp_(B1c[:, :, k], psb[pb][:, 12:14], [PS(pb)], ["B1c"])
        if k < 4:
            cp_(LBR[:, :, k], psb[pb][:, 0:2], [PS(pb)], ["LBR"])
            cp_(ONG[:, :, k], psb[pb][:, 2:4], [PS(pb)], ["ONG"])
            cp_(CWT[:, :, :, k], psb[pb][:, 4:10].rearrange("p (a b) -> p a b", a=DEPTH), [PS(pb)], ["CWT"])
            for l_ in range(DEPTH):
                cp_(SCT[:, (l_ * 4 + k) * 16:(l_ * 4 + k + 1) * 16, :],
                    psb[pb][:, 16 + l_ * 32:16 + (l_ + 1) * 32].rearrange("p (a b) -> p a b", a=16), [PS(pb)],
                    [("SCT", l_ * 4 + k)], eng="act")
    memset(LB[:, 0, :], 0.0, [], ["LB"], eng="dve")
    memset(OML[:, 0, :], 1.0, [], ["OML"], eng="dve")
    memset(LNOML[:, 0, :], 0.0, [], ["LNOML"], eng="dve")
    tt(LBT, LBR[:, 0, :], LBR[:, 1, :], ALU.subtract, ["LBR"], ["LBT"])
    act(LBR[:, 0, :], LBT, AF.Exp, ["LBT"], ["LBR"])
    act(LBR[:, 0, :], LBR[:, 0, :], AF.Ln, ["LBR"], ["LBR"], bias=1.0)
    act(LB[:, 1, :], LBR[:, 0, :], AF.Exp, ["LBR", "LB"], ["LB"], scale=-1.0)
    tt(LNOML[:, 1, :], LBT, LBR[:, 0, :], ALU.subtract, ["LBT", "LBR", "LNOML"], ["LNOML"])
    act(OML[:, 1, :], LNOML[:, 1, :], AF.Exp, ["LNOML", "OML"], ["OML"])

    for i in range(2):
        dma("sp", R[:, i, :], xp[i * 128:(i + 1) * 128, :], [], [("R", i)])

    def late_input_loads():
        for i in range(2, NPT):
            dma("sp", R[:, i, :], xp[i * 128:(i + 1) * 128, :], ["WIN4"], [("R", i)])
        dma("sp", R[:, NPT, :], xs, ["WIN4"], [("R", NPT)])

    ln_ctr = [0]

    def layer_norm_tile(i, l, final_out, affine_eng="pool", affine=True):
        k = ln_ctr[0] % 4
        ln_ctr[0] += 1
        b = LNS[k]
        lk = ("LNS", k)
        rk = ("R", i)
        Ri = R[:, i, :]
        S.add("dve", lambda e: e.bn_stats(out=b["ST"][:, 0:6], in_=Ri[:, 0:512]), [rk], [lk], cost=640.0)
        S.add("dve", lambda e: e.bn_stats(out=b["ST"][:, 6:12], in_=Ri[:, 512:1024]), [rk, lk], [lk], cost=640.0)
        S.add("dve", lambda e: e.bn_aggr(out=b["MV"], in_=b["ST"]), [lk], [lk], cost=200.0)
        act(b["LNV"], b["MV"][:, 1:2], AF.Ln, [lk], [lk], bias=LN_EPS)
        act(b["RSTD"], b["LNV"], AF.Exp, [lk], [lk], scale=-0.5)
        stt(b["NMR"], b["MV"][:, 0:1], -1.0, b["RSTD"], ALU.mult, ALU.mult, [lk], [lk])
        act(Ri, Ri, AF.Identity, [rk, lk], [rk], scale=b["RSTD"], bias=b["NMR"])
        if affine:
            tt(Ri, Ri, LNG, ALU.mult, [rk, "LNG"], [rk], eng=affine_eng, f=1.0)
            tt(Ri, Ri, LNB, ALU.add, [rk, "LNB"], [rk], eng=affine_eng, f=1.0)
        if final_out:
            if i < NPT:
                dma("sp", yp[i * 128:(i + 1) * 128, :], Ri, [rk], [], is_output=True)
            else:
                dma("sp", ys, Ri, [rk], [], is_output=True)

    def gen_xt(i, dst, dst_key, affine_l=None):
        for half in range(2):
            pb = bank("xt")
            for q in range(4):
                dk = half * 4 + q
                tr(psb[pb][:, q * 128:(q + 1) * 128], R[:, i, dk * 128:(dk + 1) * 128], ident,
                   [("R", i), "ident"], [PS(pb)])
            if affine_l is None:
                act(dst[:, half * 4:(half + 1) * 4, :], psb[pb][:, 0:512].rearrange("p (a b) -> p a b", a=4), AF.Copy,
                    [PS(pb)], [dst_key])
            else:
                for q in range(4):
                    dk = half * 4 + q
                    act(dst[:, dk, :], psb[pb][:, q * 128:(q + 1) * 128], AF.Identity, [PS(pb), "G1c", "B1c"], [dst_key],
                        scale=G1c[:, affine_l, dk:dk + 1], bias=B1c[:, affine_l, dk:dk + 1])

    for l in range(DEPTH):
        for blk in (1, 0, 3, 2, 5, 6, 4):
            dma("pool", WINB[blk], w_in[l][:, blk * 512:(blk + 1) * 512].rearrange("(k p) n -> p k n", p=128), [],
                ["WIN%d" % blk], max_dma_last_dim=8192)
        dma("pool", WOUT, w_out[l].rearrange("(k p) n -> p k n", p=128), [], ["WOUT"], max_dma_last_dim=8192)
        if l == 0:
            late_input_loads()
        memset(Sst, 0.0, [], ["Sst"], eng="dve")
        memset(Sb[0], 0.0, [], [("Sb", 0)], eng="dve")
        memset(CAR, 0.0, [], ["CAR"], eng="dve")
        sb_ctr = [0]

        supers = [("p", 2 * i, 2) for i in range(NPT // 2)] + [("s", NPT, 1)]
        for (kind, tile0, ntl) in supers:
            S.parity = (tile0 // 2) % 2
            n = 128 * ntl
            is_p = kind == "p"
            blk = 64 if is_p else 8
            G = n // blk
            gpt = 128 // blk
            RM = RMP if is_p else RMS
            GM = GMP if is_p else GMS
            mask = maskP if is_p else maskS
            for j in range(ntl):
                gen_xt(tile0 + j, XTc[:, :, j * 128:(j + 1) * 128], "XTc")
            for h in range(NH):
                tset = (h % 2) * 3
                TA, TB, TC = GT[tset], GT[tset + 1], GT[tset + 2]
                ka, kb, kc = ("GT", tset), ("GT", tset + 1), ("GT", tset + 2)
                pb = bank("fq")
                zf = psb[pb][:, 0:n]
                zq = psb[pb][:, 256:256 + n]
                for dk in range(8):
                    mm(zf, WINB[1][:, dk, h * 128:(h + 1) * 128], XTc[:, dk, 0:n], dk == 0, dk == 7,
                       ["WIN1", "XTc"], [PS(pb)])
                for dk in range(8):
                    mm(zq, WINB[0][:, dk, h * 128:(h + 1) * 128], XTc[:, dk, 0:n], dk == 0, dk == 7,
                       ["WIN0", "XTc"], [PS(pb)])
                lb_ap = LB[:, l, h:h + 1]
                oml_ap = OML[:, l, h:h + 1]
                lno_ap = LNOML[:, l, h:h + 1]
                a, b_, c = TA[:, 0:n], TB[:, 0:n], TC[:, 0:n]
                act(a, zf, AF.Exp, [PS(pb)], [ka], scale=-1.0)
                act(b_, a, AF.Ln, [ka], [kb], bias=1.0)
                act(a, b_, AF.Exp, [kb], [ka], scale=-1.0)
                act(a, a, AF.Ln, [ka, "LB", "OML"], [ka], scale=oml_ap, bias=lb_ap)
                S.add("dve", lambda e, c=c, a=a, RMv=RM[:, 0:n]: e.tensor_tensor_scan(
                    out=c, data0=RMv, data1=a, initial=0.0, op0=ALU.mult, op1=ALU.add),
                    [ka, "RMP", "RMS"], [kc], cost=100.0 + 2.1 * n)
                tt(b_, zf, b_, ALU.add, [PS(pb), kb], [kb])
                tt(b_, b_, c, ALU.add, [kb, kc], [kb])
                act(KT[:, h, 0:n], b_, AF.Exp, [kb, "LNOML"], ["KT"], scale=-1.0, bias=lno_ap)
                act(ELb[:, h, 0:G], c.rearrange("p (g c) -> p g c", c=blk)[:, :, blk - 1], AF.Exp, [kc], ["ELb"])
                act(a, zq, AF.Exp, [PS(pb)], [ka], scale=-1.0)
                act(a, a, AF.Ln, [ka], [ka], bias=1.0)
                tt(a, c, a, ALU.subtract, [kc, ka], [ka])
                act(a, a, AF.Exp, [ka], [ka])
                tt(QT[:, h, 0:n], zq, a, ALU.mult, [PS(pb), ka], ["QT"])
            for h in range(NH):
                tset = (h % 2) * 3
                TA = GT[tset]
                ka = ("GT", tset)
                a = TA[:, 0:n]
                pb = bank("g")
                zg = psb[pb][:, 0:n]
                for dk in range(8):
                    mm(zg, WINB[3][:, dk, h * 128:(h + 1) * 128], XTc[:, dk, 0:n], dk == 0, dk == 7,
                       ["WIN3", "XTc"], [PS(pb)])
                act(a, zg, AF.Exp, [PS(pb)], [ka], scale=-1.0)
                act(a, a, AF.Ln, [ka], [ka], bias=1.0)
                act(a, a, AF.Exp, [ka], [ka], scale=-1.0)
                stt(SG[:, h, 0:n], zg, ONG[:, l, h:h + 1], a, ALU.mult, ALU.mult, [PS(pb), ka, "ONG"], ["SG"])
            for j in range(ntl):
                pb = bank("v")
                zv = psb[pb][:, 0:512]
                for dk in range(8):
                    mm(zv, XTc[:, dk, j * 128:(j + 1) * 128], WINB[2][:, dk, :], dk == 0, dk == 7,
                       ["WIN2", "XTc"], [PS(pb)])
                act(VTOK[:, j, :], zv, AF.Copy, [PS(pb)], ["VTOK"])
                pb2 = bank("kt")
                ktp = psb[pb2][:, 0:256].bitcast(BF16)
                for h in range(NH):
                    tr(ktp[:, h * 128:(h + 1) * 128], KT[:, h, j * 128:(j + 1) * 128], identb, ["KT", "identb"],
                       [PS(pb2)])
                cp_(KTOK[:, j, :], ktp, [PS(pb2)], ["KTOK"], eng=KTOK_ENG)
            def rms_gate(ov, W, sgv, mixv, okeys, three_d):
                act(OSQ[:, 0:W], ov, AF.Square, okeys, ["OSQ"])
                pbQ = bank_excl("q")
                mm(psb[pbQ][:, 0:W], ones_s, OSQ[:, 0:W], True, True, ["ones_s", "OSQ"], [PS(pbQ)])
                rst = RST[:, 0:W]
                tmp = TMPO[:, 0:W]
                act(rst, psb[pbQ][:, 0:W], AF.Ln, [PS(pbQ)], [("GT", 0), ("GT", 1)], bias=RMS_EPS)
                act(rst, rst, AF.Exp, [("GT", 0), ("GT", 1)], [("GT", 0), ("GT", 1)], scale=-0.5)
                tt(tmp, ov, rst, ALU.mult, okeys + [("GT", 0), ("GT", 1)], [("GT", 2), ("GT", 3)])
                tv = tmp.rearrange("p (a b) -> p a b", a=NH) if three_d else tmp
                tt(mixv, sgv, tv, ALU.mult, ["SG", ("GT", 2), ("GT", 3)], ["MIX"])

            held = set()

            def bank_excl(cls=None):
                while True:
                    b = bank(cls if is_p else None)
                    if b not in held:
                        return b

            for j in range(ntl):
                c0, c1 = j * 128, (j + 1) * 128
                pbA = bank_excl("sc")
                for h in range(NH):
                    mm(psb[pbA][:, h * 128:(h + 1) * 128], KT[:, h, c0:c1], QT[:, h, c0:c1], True, True,
                       ["KT", "QT"], [PS(pbA)])
                at = ATm[j % 2]
                atk = ("ATm", j % 2)
                tt(at, psb[pbA][:, 0:512].rearrange("p (a b) -> p a b", a=NH),
                   mask.unsqueeze(1).to_broadcast([128, NH, 128]), ALU.mult, [PS(pbA), "maskP", "maskS"], [atk])
                if is_p:
                    grp_sb = []
                    for g in range(gpt):
                        gi = j * gpt + g
                        km = KM[g % 2]
                        kmk = ("KM", g % 2)
                        ts(km, KTOK[:, j, :], GM[:, g:g + 1], ALU.mult, ["KTOK", "GMP", "GMS"], [kmk], eng=KM_ENG)
                        pbP = bank_excl("P")
                        for h in range(NH):
                            mm(psb[pbP][:, h * 128:(h + 1) * 128], km[:, h * 128:(h + 1) * 128],
                               VTOK[:, j, h * 128:(h + 1) * 128], True, True, [kmk, "VTOK"], [PS(pbP)])
                        Pv = psb[pbP][:, 0:512].rearrange("p (a b) -> p a b", a=NH)
                        elv = ELb[:, :, gi:gi + 1].to_broadcast([128, NH, 128])
                        cur = sb_ctr[0] % 3
                        nxt = (sb_ctr[0] + 1) % 3
                        sb_ctr[0] += 1
                        grp_sb.append((Sb[cur], ("Sb", cur)))
                        tt(STMP, Pv, Sst, ALU.add, [PS(pbP), "Sst"], ["STMP"])
                        tt(Sst, STMP, elv, ALU.mult, ["STMP", "ELb"], ["Sst"])
                        cp_(Sb[nxt], Sst, ["Sst"], [("Sb", nxt)], eng="act")
                    pbO = bank_excl("o")
                    for h in range(NH):
                        oh = psb[pbO][:, h * 128:(h + 1) * 128]
                        mm(oh, VTOK[:, j, h * 128:(h + 1) * 128], at[:, h, :], True, False, ["VTOK", atk], [PS(pbO)])
                        for g in range(gpt):
                            sbv, sbk = grp_sb[g]
                            mm(oh[:, g * blk:(g + 1) * blk], sbv[:, h, :], QT[:, h, c0 + g * blk:c0 + (g + 1) * blk],
                               False, g == gpt - 1, [sbk, "QT"], [PS(pbO)])
                    rms_gate(psb[pbO][:, 0:512], 512, SG[:, :, c0:c1], MIX[:, 0:NH, c0:c1], [PS(pbO)], True)
                else:
                    pbO = bank_excl("o")
                    mm(psb[pbO][:, 0:512], zeros_b, KTOK[:, j, :], True, False, ["zeros_b", "KTOK"], [PS(pbO)])
                    for h in range(NH):
                        mm(psb[pbO][:, h * 128:(h + 1) * 128], VTOK[:, j, h * 128:(h + 1) * 128], at[:, h, :], False, False,
                           ["VTOK", atk], [PS(pbO)])

                    NSB = 4

                    def load_s0(g):
                        dma("sp", S0f[g % NSB], sh[l, g].rearrange("h k v -> k h v"), [],
                            [("S0f", g % NSB)] + (["S0GATE"] if g == 13 else []))

                    for g0 in range(NSB - 1):
                        load_s0(g0)
                    for g in range(gpt):
                        gi = j * gpt + g
                        s2 = g % 2
                        s4 = g % NSB
                        if g + NSB - 1 < gpt:
                            load_s0(g + NSB - 1)
                        cp_(S0b[s2], S0f[s4], [("S0f", s4)], [("S0b", s2)], eng="act")
                        for h in range(NH):
                            mm(psb[pbO][:, h * 128 + g * blk:h * 128 + (g + 1) * blk], S0b[s2][:, h, :],
                               QT[:, h, c0 + g * blk:c0 + (g + 1) * blk], False, g == gpt - 1 and h == NH - 1,
                               [("S0b", s2), "QT"], [PS(pbO)])
                        km = KM[g % 2]
                        kmk = ("KM", g % 2)
                        ts(km, KTOK[:, j, :], GM[:, g:g + 1], ALU.mult, ["KTOK", "GMP", "GMS"], [kmk], eng=KM_ENG)
                        pbP = bank_excl()
                        for h in range(NH):
                            mm(psb[pbP][:, h * 128:(h + 1) * 128], km[:, h * 128:(h + 1) * 128],
                               VTOK[:, j, h * 128:(h + 1) * 128], True, True, [kmk, "VTOK"], [PS(pbP)])
                        Pv = psb[pbP][:, 0:512].rearrange("p (a b) -> p a b", a=NH)
                        elv = ELb[:, :, gi:gi + 1].to_broadcast([128, NH, 128])
                        tt(STMP, Pv, S0f[s4], ALU.add, [PS(pbP), ("S0f", s4)], ["STMP"])
                        tt(S0f[s4], STMP, elv, ALU.mult, ["STMP", "ELb"], [("S0f", s4)])
                        dma("pool", hs[l, g].rearrange("h k v -> k h v"), S0f[s4], [("S0f", s4)], [], is_output=True)
                    rms_gate(psb[pbO][:, 0:512], 512, SG[:, :, c0:c1], MIX[:, 0:NH, c0:c1], [PS(pbO)], True)
            if is_p and tile0 + ntl == NPT:
                dma("sp", hp[l].rearrange("h k v -> k h v"), Sst, ["Sst"], [], is_output=True)

            ZH = GT[4][:, 0:n]
            ACC = GT[5][:, 0:n]
            kzh, kacc = ("GT", 4), ("GT", 5)
            for cc in range(4):
                pb = bank("cv")
                pb2 = bank("cv")
                zC = psb[pb][:, 0:n]
                zh = psb[pb][:, 256:256 + n]
                zB = psb[pb2][:, 0:n]
                for (zz, blk, pbx) in ((zC, 5, pb), (zh, 6, pb), (zB, 4, pb2)):
                    for dk in range(8):
                        mm(zz, WINB[blk][:, dk, cc * 128:(cc + 1) * 128], XTc[:, dk, 0:n], dk == 0, dk == 7,
                           ["WIN%d" % blk, "XTc"], [PS(pbx)])
                act(ZH, zh, AF.Copy, [PS(pb)], [kzh])
                CWK = ["CWT"]
                w0 = CWT[:, l, 0, cc:cc + 1]
                w1 = CWT[:, l, 1, cc:cc + 1]
                w2 = CWT[:, l, 2, cc:cc + 1]
                if is_p:
                    cp_(U[:, 0:2], CAR[:, cc, :], ["CAR"], ["U"], eng=CAR_ENG)
                    tt(U[:, 2:2 + n], zC, ZH, ALU.mult, [PS(pb), kzh, "U"], ["U"])
                    ts(ACC, U[:, 2:2 + n], w2, ALU.mult, ["U"] + CWK, [kacc])
                    stt(ACC, U[:, 1:1 + n], w1, ACC, ALU.mult, ALU.add, ["U", kacc] + CWK, [kacc])
                    stt(ACC, U[:, 0:n], w0, ACC, ALU.mult, ALU.add, ["U", kacc] + CWK, [kacc])
                    cp_(CAR[:, cc, :], U[:, n:n + 2], ["U"], ["CAR"], eng=CAR_ENG)
                else:
                    U3 = U[:, 0:160].rearrange("p (b t) -> p b t", t=10)
                    v3 = lambda ap: ap.rearrange("p (b t) -> p b t", t=8)
                    cp_(U3[:, :, 0:2], SCT[:, (l * 4 + cc) * 16:(l * 4 + cc + 1) * 16, :], [("SCT", l * 4 + cc)], ["U"])
                    tt(U3[:, :, 2:10], v3(zC), v3(ZH), ALU.mult, [PS(pb), kzh, "U"], ["U"])
                    ts(v3(ACC), U3[:, :, 2:10], w2, ALU.mult, ["U"] + CWK, [kacc])
                    stt(v3(ACC), U3[:, :, 1:9], w1, v3(ACC), ALU.mult, ALU.add, ["U", kacc] + CWK, [kacc])
                    stt(v3(ACC), U3[:, :, 0:8], w0, v3(ACC), ALU.mult, ALU.add, ["U", kacc] + CWK, [kacc])
                    cp_(CTAIL[:, cc * 16:(cc + 1) * 16, :], U3[:, :, 8:10], ["U"], ["CTAIL"])
                tt(MIX[:, NH + cc, 0:n], zB, ACC, ALU.mult, [PS(pb2), kacc], ["MIX"])
            if is_p and tile0 + ntl == NPT:
                pb = bank("xt")
                for cc in range(4):
                    tr(psb[pb][0:2, cc * 128:(cc + 1) * 128], CAR[:, cc, :], ident, ["CAR", "ident"], [PS(pb)])
                stg = GTALL[0:2, 0:512]
                cp_(stg, psb[pb][0:2, 0:512], [PS(pb)], [("GT", 0), ("GT", 1)])
                dma("sp", cp[l], stg, [("GT", 0), ("GT", 1)], [], is_output=True)
            if not is_p:
                pb = bank("xt")
                for cc in range(4):
                    tr(psb[pb][0:32, cc * 128:(cc + 1) * 128],
                       CTAIL[:, cc * 16:(cc + 1) * 16, :].rearrange("p a b -> p (a b)"), ident, ["CTAIL", "ident"],
                       [PS(pb)])
                stg = GTALL[0:32, 1024:1536]
                cp_(stg, psb[pb][0:32, 0:512], [PS(pb)], [("GT", 4), ("GT", 5)])
                dma("sp", cs[l].rearrange("b j c -> (b j) c"), stg, [("GT", 4), ("GT", 5)], [], is_output=True)

            if "mix" in taps and l == 0 and not is_p:
                dma("pool", dbg_mix, MIX[:, :, 0:128], ["MIX"], [], is_output=True)
                dma("pool", dbg_sg, SG[:, :, 0:128], ["SG"], [], is_output=True)
                dma("pool", dbg_qt, QT[:, :, 0:128], ["QT"], [], is_output=True)
                dma("pool", dbg_kt, KT[:, :, 0:128], ["KT"], [], is_output=True)
            for j in range(ntl):
                i = tile0 + j
                for half in range(2):
                    pb = bank("wo")
                    for mk in range(8):
                        mm(psb[pb][:, 0:512], MIX[:, mk, j * 128:(j + 1) * 128], WOUT[:, mk, half * 512:(half + 1) * 512],
                           mk == 0, mk == 7, ["MIX", "WOUT"], [PS(pb)])
                    Rh = R[:, i, half * 512:(half + 1) * 512]
                    stt(Rh, Rh, ALPHA, psb[pb][:, 0:512], ALU.mult, ALU.add, [("R", i), PS(pb)], [("R", i)])
                layer_norm_tile(i, l, False, affine=False)

        S.parity = None
        if ("m%d" % l) in taps:
            for i in range(NT):
                dma("sp", tap_out["m%d" % l][i], R[:, i, :], [("R", i)], [], is_output=True)

        dma("sp", LNG, ln1_g[l:l + 1, :].to_broadcast([128, D]), [], ["LNG"])
        dma("sp", LNB, ln1_b[l:l + 1, :].to_broadcast([128, D]), [], ["LNB"])
        ts(LNG, LNG, ALPHA, ALU.mult, ["LNG"], ["LNG"])
        ts(LNB, LNB, ALPHA, ALU.mult, ["LNB"], ["LNB"])

        def load_ffn(c):
            buf = c % 3
            gate = []
            dma("pool", F1[buf], w_ff1[l][:, c * 512:(c + 1) * 512].rearrange("(k p) n -> p k n", p=128), gate,
                ["F1_%d" % buf], max_dma_last_dim=8192)
            dma("pool", F2[buf], w_ff2[l][c * 512:(c + 1) * 512, :].rearrange("(k p) n -> p k n", p=128), gate,
                ["F2_%d" % buf], max_dma_last_dim=8192)

        def xt_rhs(tile0_, ntl_, dk_):
            if ntl_ == 1:
                return XTt[tile0_][:, dk_, :]
            base = XTt[tile0_]
            flat = arena[:, 0:ARENA_BYTES // 4].bitcast(BF16)
            off = xt_off[tile0_] + dk_ * 128
            return flat[:, off:off + ntl_ * 1024].rearrange("p (t r) -> p t r", r=1024)[:, :, 0:128]

        fsup = []
        t = 0
        while t < NPT:
            k = min(4, NPT - t)
            fsup.append((t, k))
            t += k
        fsup.append((NPT, 1))
        hctr = [0]
        load_ffn(0)
        load_ffn(1)
        for c in range(8):
            if c + 2 < 8:
                load_ffn(c + 2)
            if c == 1:
                dma("sp", LNG, ln2_g[l:l + 1, :].to_broadcast([128, D]), [], ["LNG"])
                dma("sp", LNB, ln2_b[l:l + 1, :].to_broadcast([128, D]), [], ["LNB"])
            buf = c % 3
            sup_c = fsup if c < 7 else [(NPT, 1)] + [(t_, 2) for t_ in range(0, NPT, 2)]
            for (tile0, ntl) in sup_c:
                n = 128 * ntl
                t0 = tile0 * 128
                if c == 0:
                    for j in range(ntl):
                        gen_xt(tile0 + j, XTt[tile0 + j], "XT%d" % (tile0 + j), affine_l=l)
                xtk = ["XT%d" % (tile0 + j) for j in range(ntl)]
                hb = HID[hctr[0] % 2]
                hbk = ("HID", hctr[0] % 2)
                hctr[0] += 1
                for fc in range(4):
                    pb = bank("f1")
                    for dk in range(8):
                        mm(psb[pb][:, 0:n], F1[buf][:, dk, fc * 128:(fc + 1) * 128], xt_rhs(tile0, ntl, dk), dk == 0,
                           dk == 7, ["F1_%d" % buf] + xtk, [PS(pb)])
                    sq = SQ[fc % 2]
                    sqk = ("SQ", fc % 2)
                    act(sq[:, 0:n], psb[pb][:, 0:n], AF.Square, [PS(pb)], [sqk])
                    stt(hb[:, fc, 0:n], psb[pb][:, 0:n], 0.0, sq[:, 0:n], ALU.is_gt, ALU.mult, [PS(pb), sqk], [hbk])
                for j in range(ntl):
                    i = tile0 + j
                    if c == 0:
                        tt(R[:, i, :], R[:, i, :], LNG, ALU.mult, [("R", i), "LNG"], [("R", i)], f=1.0)
                        tt(R[:, i, :], R[:, i, :], LNB, ALU.add, [("R", i), "LNB"], [("R", i)], f=1.0)
                    for half in range(2):
                        pb = bank("f2")
                        for fk in range(4):
                            mm(psb[pb][:, 0:512], hb[:, fk, j * 128:(j + 1) * 128],
                               F2[buf][:, fk, half * 512:(half + 1) * 512], fk == 0, fk == 3, [hbk, "F2_%d" % buf],
                               [PS(pb)])
                        Rh = R[:, i, half * 512:(half + 1) * 512]
                        tt(Rh, psb[pb][:, 0:512], Rh, ALU.add, [("R", i), PS(pb)], [("R", i)])
                    if c == 7:
                        layer_norm_tile(i, l, l == DEPTH - 1, affine_eng=LN2_ENG[i % len(LN2_ENG)])
        if ("f%d" % l) in taps:
            for i in range(NT):
                dma("sp", tap_out["f%d" % l][i], R[:, i, :], [("R", i)], [], is_output=True)

    if SCHEDULE:
        S.schedule()
        print("[kernel] simulated time (us):", S.sim_time / 1e3)
    stats = S.emit(nc, sems, dma_sems)
    print("[kernel] ops (instr, waits):", stats)
    es.close()
    return nc


_PROG_CACHE = {}


def _get_prog(n_ptiles=16, taps=()):
    key = (n_ptiles, tuple(taps))
    if key not in _PROG_CACHE:
        _PROG_CACHE[key] = build_program(n_ptiles, taps)
    return _PROG_CACHE[key]


def _in_maps(inp, n_ptiles=16):
    seql = n_ptiles * 128
    f = lambda a: np.ascontiguousarray(np.asarray(a, dtype=np.float32))
    shared = {k: f(inp[k]) for k in ("w_in", "lb_logits", "conv_w", "onorm_g", "w_out", "ln1_g", "ln1_b", "w_ff1",
                                     "w_ff2", "ln2_g", "ln2_b")}
    maps = []
    for c in range(N_CORES):
        m = dict(shared)
        m["xp"] = f(inp["x_prompt"][c, :seql])
        m["xs"] = f(np.asarray(inp["x_sample"])[16 * c:16 * (c + 1)].reshape(128, D))
        m["sh"] = f(np.asarray(inp["state_hgrn"])[:, 16 * c:16 * (c + 1)])
        m["sc"] = f(np.asarray(inp["state_conv"])[:, 16 * c:16 * (c + 1)])
        maps.append(m)
    return maps


def kernel(**inputs):
    nc = _get_prog()
    res = run_bass_kernel_spmd(nc, _in_maps(inputs), core_ids=list(range(N_CORES)))
    r = res.results
    y_prompt = np.stack([r[c]["yp"] for c in range(N_CORES)], axis=0).astype(np.float32)
    y_sample = np.concatenate([r[c]["ys"].reshape(16, DEC_T, D) for c in range(N_CORES)], axis=0).astype(np.float32)
    new_hgrn_prompt = np.stack([r[c]["hp"] for c in range(N_CORES)], axis=1).astype(np.float32)
    new_conv_prompt = np.stack([r[c]["cp"] for c in range(N_CORES)], axis=1).astype(np.float32)
    new_hgrn_sample = np.concatenate([r[c]["hs"] for c in range(N_CORES)], axis=1).astype(np.float32)
    new_conv_sample = np.concatenate([r[c]["cs"] for c in range(N_CORES)], axis=1).astype(np.float32)
    return (y_prompt, y_sample, new_hgrn_prompt, new_conv_prompt, new_hgrn_sample, new_conv_sample)
```
